# Optimizing a Trainium2 kernel written in Bass

```python
import math
import jax, jax.numpy as jnp
from jax import lax
import numpy as np

D_MODEL = 1024
BATCH = 16
SEQ = 2048
DEPTH = 4
DEC_BATCH = 8
DEC_SEQ = 4096
PAST_LEN = 128

N_DIFF_HEADS = 4
DIFF_HEAD_DIM = 64
DIFF_WIDTH = N_DIFF_HEADS * 2 * DIFF_HEAD_DIM
FOURIER_WIDTH = D_MODEL - DIFF_WIDTH
N_FOURIER_GROUPS = 4
FOURIER_GROUP_DIM = FOURIER_WIDTH // N_FOURIER_GROUPS
MIX_WIDTH = DIFF_WIDTH + FOURIER_WIDTH
IN_WIDTH = 3 * DIFF_WIDTH + FOURIER_WIDTH
D_FF = -(-8 * D_MODEL // (3 * 256)) * 256
ROPE_THETA = 10000.0
NORM_EPS = 1e-6
SUBLN_EPS = 1e-5
Q_BLOCK = 128

kernel_name = "hybrid_diffattn_fnet_encoder"


def rmsnorm(x, g, eps=NORM_EPS):
    xf = x.astype(jnp.float32)
    y = xf * lax.rsqrt(jnp.mean(xf * xf, axis=-1, keepdims=True) + eps)
    return (y * g.astype(jnp.float32)).astype(x.dtype)


def lambda_init_fn(layer_idx):
    return 0.8 - 0.6 * math.exp(-0.3 * layer_idx)


def rope_tables(seq_len):
    inv = 1.0 / (ROPE_THETA ** (jnp.arange(0, DIFF_HEAD_DIM, 2, dtype=jnp.float32) / DIFF_HEAD_DIM))
    ang = jnp.arange(seq_len, dtype=jnp.float32)[:, None] * inv[None, :]
    return jnp.cos(ang), jnp.sin(ang)


def apply_rope(x, cos, sin):
    xf = x.astype(jnp.float32)
    x1, x2 = jnp.split(xf, 2, axis=-1)
    c = cos[None, :, None, :]
    s = sin[None, :, None, :]
    out = jnp.concatenate([x1 * c - x2 * s, x2 * c + x1 * s], axis=-1)
    return out.astype(x.dtype)


def diff_attention(q, k, v, lam, subln_g, lam_init):
    B, S = q.shape[0], q.shape[1]
    nb = S // Q_BLOCK
    q = q * jnp.asarray(DIFF_HEAD_DIM ** -0.5, q.dtype)
    qb = q.reshape(B, nb, Q_BLOCK, 2 * N_DIFF_HEADS, DIFF_HEAD_DIM).transpose(1, 0, 2, 3, 4)
    vf = v.astype(jnp.float32)

    def block(q_blk):
        s = jnp.einsum('bqnd,bknd->bnqk', q_blk, k, preferred_element_type=jnp.float32)
        p = jax.nn.softmax(s, axis=-1).reshape(B, N_DIFF_HEADS, 2, Q_BLOCK, S)
        a = p[:, :, 0] - lam * p[:, :, 1]
        return jnp.einsum('bhqk,bkhe->bqhe', a, vf)

    o = lax.map(block, qb)
    o = o.transpose(1, 0, 2, 3, 4).reshape(B, S, N_DIFF_HEADS, 2 * DIFF_HEAD_DIM)
    o = rmsnorm(o, subln_g, SUBLN_EPS).astype(jnp.float32) * (1.0 - lam_init)
    return o.reshape(B, S, DIFF_WIDTH).astype(v.dtype)


def fourier_mix(f):
    B, S = f.shape[0], f.shape[1]
    u = f.astype(jnp.float32).reshape(B, S, N_FOURIER_GROUPS, FOURIER_GROUP_DIM)
    z = jnp.fft.fftn(u, axes=(1, 3), norm='ortho').real
    return z.reshape(B, S, FOURIER_WIDTH).astype(f.dtype)


def trunk(x, norm1_g, w_in, lambda_q1, lambda_k1, lambda_q2, lambda_k2, subln_g,
          w_out, norm2_g, w_gate, w_up, w_down, final_g):
    B, S, _ = x.shape
    cos, sin = rope_tables(S)
    for i in range(DEPTH):
        lam_init = lambda_init_fn(i)
        h = rmsnorm(x, norm1_g[i])
        proj = h @ w_in[i]
        q = proj[..., :DIFF_WIDTH].reshape(B, S, 2 * N_DIFF_HEADS, DIFF_HEAD_DIM)
        k = proj[..., DIFF_WIDTH:2 * DIFF_WIDTH].reshape(B, S, 2 * N_DIFF_HEADS, DIFF_HEAD_DIM)
        v = proj[..., 2 * DIFF_WIDTH:3 * DIFF_WIDTH].reshape(B, S, N_DIFF_HEADS, 2 * DIFF_HEAD_DIM)
        f = proj[..., 3 * DIFF_WIDTH:]
        q = apply_rope(q, cos, sin)
        k = apply_rope(k, cos, sin)
        lam = (jnp.exp(jnp.sum(lambda_q1[i].astype(jnp.float32) * lambda_k1[i].astype(jnp.float32)))
               - jnp.exp(jnp.sum(lambda_q2[i].astype(jnp.float32) * lambda_k2[i].astype(jnp.float32)))
               + lam_init)
        a = diff_attention(q, k, v, lam, subln_g[i], lam_init)
        fo = fourier_mix(f)
        x = x + jnp.concatenate([a, fo], axis=-1) @ w_out[i]
        h2 = rmsnorm(x, norm2_g[i])
        x = x + (jax.nn.silu(h2 @ w_gate[i]) * (h2 @ w_up[i])) @ w_down[i]
    return rmsnorm(x, final_g)


def setup_inputs(seed: int = 0) -> dict:
    key = jax.random.key(seed)
    ks = jax.random.split(key, 16)
    f32 = jnp.float32
    nrm = lambda k, shape, scale: jax.random.normal(k, shape, f32) * scale
    return {
        "x_prompt": nrm(ks[0], (BATCH, SEQ, D_MODEL), 1.0),
        "x_sample": nrm(ks[1], (DEC_BATCH, DEC_SEQ, D_MODEL), 1.0),
        "norm1_g": 1.0 + nrm(ks[2], (DEPTH, D_MODEL), 0.02),
        "w_in": nrm(ks[3], (DEPTH, D_MODEL, IN_WIDTH), D_MODEL ** -0.5),
        "lambda_q1": nrm(ks[4], (DEPTH, DIFF_HEAD_DIM), 0.1),
        "lambda_k1": nrm(ks[5], (DEPTH, DIFF_HEAD_DIM), 0.1),
        "lambda_q2": nrm(ks[6], (DEPTH, DIFF_HEAD_DIM), 0.1),
        "lambda_k2": nrm(ks[7], (DEPTH, DIFF_HEAD_DIM), 0.1),
        "subln_g": 1.0 + nrm(ks[8], (DEPTH, 2 * DIFF_HEAD_DIM), 0.02),
        "w_out": nrm(ks[9], (DEPTH, MIX_WIDTH, D_MODEL), MIX_WIDTH ** -0.5),
        "norm2_g": 1.0 + nrm(ks[10], (DEPTH, D_MODEL), 0.02),
        "w_gate": nrm(ks[11], (DEPTH, D_MODEL, D_FF), D_MODEL ** -0.5),
        "w_up": nrm(ks[12], (DEPTH, D_MODEL, D_FF), D_MODEL ** -0.5),
        "w_down": nrm(ks[13], (DEPTH, D_FF, D_MODEL), D_FF ** -0.5),
        "final_g": 1.0 + nrm(ks[14], (D_MODEL,), 0.02),
    }


def reference(x_prompt, x_sample, norm1_g, w_in, lambda_q1, lambda_k1, lambda_q2, lambda_k2,
              subln_g, w_out, norm2_g, w_gate, w_up, w_down, final_g):
    y_prompt = trunk(x_prompt, norm1_g, w_in, lambda_q1, lambda_k1, lambda_q2, lambda_k2,
                     subln_g, w_out, norm2_g, w_gate, w_up, w_down, final_g)
    y_sample = trunk(x_sample, norm1_g, w_in, lambda_q1, lambda_k1, lambda_q2, lambda_k2,
                     subln_g, w_out, norm2_g, w_gate, w_up, w_down, final_g)
    return (y_prompt, y_sample)
```

```python
import math
import numpy as np
import ml_dtypes
import concourse.bass as bass
import concourse.mybir as mybir
from concourse.bass_utils import run_bass_kernel_spmd

F32 = mybir.dt.float32
BF16 = mybir.dt.bfloat16
ALU = mybir.AluOpType
AF = mybir.ActivationFunctionType

D_MODEL = 1024
D_FF = 2816
NJ = D_FF // 128
NORM_EPS = 1e-6
SUBLN_EPS = 1e-5
BLOCKNAME = {"pe": "tensor", "act": "scalar", "dve": "vector", "pool": "gpsimd", "sp": "sync"}


class Buf:
    def __init__(self, prog, name, dma=False, excl=False):
        self.name = name
        self.excl = excl
        self.w = {}
        self.r = {}
        self.sem = None
        self.count = 0
        if dma:
            self.sem = prog.nc.semaphore("d_" + name).__enter__()
            prog.dma_bufs.append(self)


class Prog:
    ENG = ["pe", "act", "dve", "pool", "sp"]
    EPOCH = 30000

    def __init__(self, nc):
        self.nc = nc
        self.ops = {e: [] for e in self.ENG}
        self.sem = {e: nc.semaphore("e_" + e).__enter__() for e in ["pe", "act", "dve", "pool"]}
        self.cnt = {e: 0 for e in self.sem}
        self.epoch = {e: 0 for e in self.sem}
        self.last = {e: {} for e in self.sem}
        self.known = {e: {} for e in self.ENG}
        self.pending = {e: {} for e in self.ENG}
        self.dma_bufs = []
        self.bufs = {}

    def buf(self, name, dma=False, excl=False):
        if name not in self.bufs:
            self.bufs[name] = Buf(self, name, dma, excl)
        return self.bufs[name]

    def op(self, eng, fn, reads=(), writes=(), dma=None, acc=False):
        waits = dict(self.pending[eng])
        self.pending[eng] = {}

        def need(evs):
            for k, (s, v) in evs.items():
                if k not in waits or waits[k][1] < v:
                    waits[k] = (s, v)

        xr = [b for b in reads if b.excl]
        reads = [b for b in reads if not b.excl]
        writes = list(writes) + xr
        for b in reads:
            need(b.w)
        for b in writes:
            if b.excl:
                need(b.r)
                need(b.w)
            elif b.r:
                need(b.r)
            elif not acc:
                need(b.w)
        final = []
        for k, (s, v) in waits.items():
            if eng == "pe" and isinstance(k, tuple) and k[0] == "pe":
                continue
            if self.known[eng].get(k, 0) >= v:
                continue
            self.known[eng][k] = v
            final.append((s, v))
        if dma is None:
            if self.cnt[eng] >= self.EPOCH:
                self.epoch[eng] += 1
                self.cnt[eng] = 0
                self.sem[eng] = self.nc.semaphore("e_%s%d" % (eng, self.epoch[eng])).__enter__()
            self.cnt[eng] += 1
            key, sem, val, inc = (eng, self.epoch[eng]), self.sem[eng], self.cnt[eng], 1
            self.last[eng][key] = (sem, val)
        else:
            dma.count += 16
            key, sem, val, inc = id(dma), dma.sem, dma.count, 16
        for b in reads:
            b.r[key] = (sem, val)
        for b in writes:
            if b.excl:
                b.r = {}
                b.w = {key: (sem, val)}
                continue
            if b.r:
                b.r = {}
                b.w = {}
            elif not acc:
                b.w = {}
            b.w[key] = (sem, val)
        self.ops[eng].append((final, fn, sem, inc))

    def barrier(self):
        evs = {}
        for e in self.sem:
            evs.update(self.last[e])
        for b in self.dma_bufs:
            if b.count > 0:
                evs[id(b)] = (b.sem, b.count)
        for e in self.ENG:
            for k, (s, v) in evs.items():
                if k not in self.pending[e] or self.pending[e][k][1] < v:
                    self.pending[e][k] = (s, v)

    def emit(self):
        nc = self.nc
        self.barrier()
        fin = [(s, v) for k, (s, v) in self.pending["sp"].items() if self.known["sp"].get(k, 0) < v]
        with nc.Block() as block:
            for eng in self.ENG:
                ops = self.ops[eng]

                def body(e, ops=ops, eng=eng):
                    for waits, fn, sem, inc in ops:
                        for s, v in waits:
                            e.wait_ge(s, v)
                        fn(e).then_inc(sem, inc)
                    if eng == "sp":
                        for s, v in fin:
                            e.wait_ge(s, v)

                getattr(block, BLOCKNAME[eng])(body)


class Arena:
    def __init__(self, ap, n):
        self.ap, self.n, self.off = ap, n, 0

    def reset(self):
        self.off = 0

    def _raw(self, n):
        n = (n + 15) // 16 * 16
        a = self.off
        self.off += n
        assert self.off <= self.n, ("arena overflow", self.off, self.n)
        return self.ap[:, a:a + n]

    def bf(self, *shape):
        n = int(np.prod(shape))
        v = self._raw(n)[:, 0:n]
        if len(shape) == 2:
            v = v.rearrange("p (a b) -> p a b", a=shape[0])
        elif len(shape) == 3:
            v = v.rearrange("p (a b c) -> p a b c", a=shape[0], b=shape[1])
        return v

    def f32(self, *shape):
        n = int(np.prod(shape))
        v = self._raw(2 * n)[:, 0:2 * n].bitcast(F32)
        if len(shape) == 2:
            v = v.rearrange("p (a b) -> p a b", a=shape[0])
        return v


def lambda_init_fn(layer_idx):
    return 0.8 - 0.6 * math.exp(-0.3 * layer_idx)


def build_program(depth, jobspec, smax):
    nc = bass.Bass("TRN2", target_bir_lowering=False)
    P = Prog(nc)

    def din(name, shape, dt=F32):
        return nc.dram_tensor(name, list(shape), dt, kind="ExternalInput").ap()

    def dscr(name, shape, dt=BF16):
        return nc.dram_tensor(name, list(shape), dt, kind="Internal").ap()

    xin, yout, rows = {}, {}, {}
    for (nm, r0, S) in jobspec:
        rows[nm] = max(rows.get(nm, 0), r0 + S)
    for nm, r in rows.items():
        xin[nm] = din(nm, [r, D_MODEL])
        yout[nm] = nc.dram_tensor("y_" + nm, [r, D_MODEL], F32, kind="ExternalOutput").ap()
    w_inx = din("w_inx", [depth, D_MODEL, 2048])
    w_out = din("w_out", [depth, D_MODEL, D_MODEL])
    w_gate = din("w_gate", [depth, D_MODEL, D_FF])
    w_up = din("w_up", [depth, D_MODEL, D_FF])
    w_down = din("w_down", [depth, D_FF, D_MODEL])
    n1g = din("n1g", [depth, D_MODEL])
    n2g = din("n2g", [depth, D_MODEL])
    fg = din("fg", [D_MODEL])
    lamv_d = din("lamv", [depth * 4 * 64])
    sg_d = din("sg", [128, depth])
    rcos = din("rcos", [128, smax])
    rsin = din("rsin", [128, smax])
    chc_d = din("chc", [128, 128], BF16)
    chs_d = din("chs", [128, 128], BF16)
    svals = sorted(set(S for _, _, S in jobspec))
    dft = {S: din("dft%d" % S, [4, S // 2, S // 2], BF16) for S in svals}

    tot_rows = sum(S for _, _, S in jobspec)
    xres = dscr("xres", [tot_rows, D_MODEL], F32)
    q_d = dscr("q_d", [4, 128, smax])
    k_d = dscr("k_d", [4, 128, smax])
    v_d = dscr("v_d", [smax, 512])
    u_d = dscr("u_d", [smax, 512])
    mix_d = dscr("mix_d", [8, 128, smax])
    wgu_b = dscr("wgu_b", [depth, NJ, 128, 2, 8, 128])

    def sb(name, shape, dt):
        return nc.alloc_sbuf_tensor(name, list(shape), dt).ap()

    ident = sb("ident", [128, 128], BF16)
    identf = sb("identf", [128, 128], F32)
    ones = sb("ones", [128, 128], BF16)
    onesf = sb("onesf", [128, 128], F32)
    chc = sb("chc_s", [128, 128], BF16)
    chs = sb("chs_s", [128, 128], BF16)
    g1b = sb("g1b", [128, D_MODEL], F32)
    g2b = sb("g2b", [128, D_MODEL], F32)
    gfb = sb("gfb", [128, D_MODEL], F32)
    lamv = sb("lamv_s", [128, depth * 4 * 64], F32)
    lamt = sb("lamt", [128, depth * 4 * 64 // 2], F32)
    lams = sb("lams", [128, depth * 2], F32)
    neglam = sb("neglam", [128, depth], F32)
    gs = sb("gs", [128, depth], F32)
    sgv = sb("sgv", [128, depth], F32)
    small = sb("small", [128, 64], F32)
    ARENA_N = 92 * 1024
    arena = Arena(sb("arena", [128, ARENA_N], BF16), ARENA_N)
    bank2 = [nc.alloc_psum_tensor("bank2_%d" % i, [128, 1024], F32).ap() for i in range(4)]
    banks = [bank2[i // 2][:, (i % 2) * 512:(i % 2 + 1) * 512] for i in range(8)]
    Bk = [P.buf("bank%d" % i, excl=True) for i in range(8)]

    B_ident, B_ones = P.buf("ident"), P.buf("ones")
    B_chc, B_chs = P.buf("chc", dma=True), P.buf("chs", dma=True)
    B_g1b, B_g2b, B_gfb = P.buf("g1b", dma=True), P.buf("g2b", dma=True), P.buf("gfb", dma=True)
    B_lam, B_sgv = P.buf("lamv", dma=True), P.buf("sgv", dma=True)
    B_small = [P.buf("small%d" % i) for i in range(16)]
    small_ctr = [0]

    def smallcol():
        i = small_ctr[0] % 16
        small_ctr[0] += 1
        return small[:, 4 * i:4 * i + 4], B_small[i]

    def dma(eng, out, in_, reads, writes, sem, acc=False):
        P.op(eng, lambda e: e.dma_start(out=out, in_=in_), reads=reads, writes=writes, dma=sem, acc=acc)

    P.op("pool", lambda e: e.memset(identf, 0.0), writes=[B_ident])
    P.op("pool", lambda e: e.affine_select(out=identf, in_=identf, pattern=[[-1, 128]], compare_op=ALU.not_equal,
                                           fill=1.0, base=0, channel_multiplier=1), reads=[B_ident], writes=[B_ident])
    P.op("dve", lambda e: e.tensor_copy(out=ident, in_=identf), reads=[B_ident], writes=[B_ident])
    P.op("dve", lambda e: e.memset(ones, 1.0), writes=[B_ones])
    P.op("dve", lambda e: e.memset(onesf, 1.0), writes=[B_ones], acc=True)
    dma("sp", chc, chc_d, [], [B_chc], B_chc)
    dma("sp", chs, chs_d, [], [B_chs], B_chs)
    dma("sp", gfb, fg.partition_broadcast(128), [], [B_gfb], B_gfb)
    dma("sp", lamv, lamv_d.partition_broadcast(128), [], [B_lam], B_lam)
    dma("sp", sgv, sg_d, [], [B_sgv], B_sgv)
    lv = lamv.rearrange("p (l f d) -> p l f d", l=depth, f=4)
    lt = lamt.rearrange("p (l f d) -> p l f d", l=depth, f=2)
    for l in range(depth):
        for f in range(2):
            P.op("dve", lambda e, l=l, f=f: e.tensor_tensor(out=lt[:, l, f, :], in0=lv[:, l, 2 * f, :], in1=lv[:, l, 2 * f + 1, :],
                                                            op=ALU.mult), reads=[B_lam], writes=[B_lam])
            P.op("dve", lambda e, l=l, f=f: e.reduce_sum(out=lams[:, 2 * l + f:2 * l + f + 1], in_=lt[:, l, f, :],
                                                         axis=mybir.AxisListType.X), reads=[B_lam], writes=[B_lam])
    P.op("act", lambda e: e.activation(out=lams, in_=lams, func=AF.Exp), reads=[B_lam], writes=[B_lam])
    for l in range(depth):
        li = lambda_init_fn(l)
        P.op("dve", lambda e, l=l, li=li: e.scalar_tensor_tensor(out=neglam[:, l:l + 1], in0=lams[:, 2 * l + 1:2 * l + 2], scalar=-li,
                                                                 in1=lams[:, 2 * l:2 * l + 1], op0=ALU.add, op1=ALU.subtract),
             reads=[B_lam], writes=[B_lam])
        P.op("dve", lambda e, l=l, li=li: e.tensor_scalar(out=gs[:, l:l + 1], in0=sgv[:, l:l + 1], scalar1=1.0 - li, scalar2=None,
                                                          op0=ALU.mult), reads=[B_sgv, B_lam], writes=[B_lam])

    B_wgu_d = P.buf("wgu_d")
    B_prep = P.buf("prep", dma=True)

    def prep_layer(l):
        for j in range(NJ):
            dma("pool", wgu_b[l, j, :, 0, :, :], w_gate[l, :, j * 128:(j + 1) * 128].rearrange("(kc p) n -> p kc n", p=128), [], [B_wgu_d], B_prep, acc=True)
            dma("pool", wgu_b[l, j, :, 1, :, :], w_up[l, :, j * 128:(j + 1) * 128].rearrange("(kc p) n -> p kc n", p=128), [], [B_wgu_d], B_prep, acc=True)

    def rstd_from(ss_ap, Bss, scale, eps):
        col, Bc = smallcol()
        P.op("act", lambda e: e.activation(out=col[:, 0:1], in_=ss_ap, func=AF.Ln, scale=scale, bias=epsb[eps]), reads=[Bss, B_eps], writes=[Bc])
        P.op("act", lambda e: e.activation(out=col[:, 1:2], in_=col[:, 0:1], func=AF.Exp, scale=-0.5), reads=[Bc], writes=[Bc])
        return col[:, 1:2], Bc

    epst = sb("epst", [128, 2], F32)
    B_eps = P.buf("eps")
    P.op("dve", lambda e: e.memset(epst[:, 0:1], NORM_EPS), writes=[B_eps])
    P.op("dve", lambda e: e.memset(epst[:, 1:2], SUBLN_EPS), writes=[B_eps], acc=True)
    epsb = {NORM_EPS: epst[:, 0:1], SUBLN_EPS: epst[:, 1:2]}

    def norm_tile(xt, Bx, gb, Bg, xn, Bxn, junk, Bj):
        col, Bc = smallcol()
        P.op("act", lambda e: e.activation(out=junk, in_=xt, func=AF.Square, accum_out=col[:, 0:1]), reads=[Bx], writes=[Bj, Bc])
        P.op("act", lambda e: e.activation(out=col[:, 1:2], in_=col[:, 0:1], func=AF.Ln, scale=1.0 / D_MODEL, bias=epsb[NORM_EPS]),
             reads=[Bc, B_eps], writes=[Bc])
        P.op("act", lambda e: e.activation(out=col[:, 2:3], in_=col[:, 1:2], func=AF.Exp, scale=-0.5), reads=[Bc], writes=[Bc])
        P.op("dve", lambda e: e.scalar_tensor_tensor(out=xn, in0=xt, scalar=col[:, 2:3], in1=gb, op0=ALU.mult, op1=ALU.mult),
             reads=[Bx, Bc, Bg], writes=[Bxn])

    def transpose_tile(xn, Bxn, bank_i, dst, Bdst, eng):
        psT = banks[bank_i].bitcast(BF16)
        for c in range(8):
            P.op("pe", lambda e, c=c: e.transpose(out=psT[:, c * 128:(c + 1) * 128], in_=xn[:, c * 128:(c + 1) * 128], identity=ident),
                 reads=[Bxn, B_ident], writes=[Bk[bank_i]], acc=True)
        src = psT.rearrange("p (c n) -> p c n", c=8)
        if eng == "act":
            P.op("act", lambda e: e.activation(out=dst, in_=src, func=AF.Copy), reads=[Bk[bank_i]], writes=[Bdst], acc=True)
        else:
            P.op("dve", lambda e: e.tensor_copy(out=dst, in_=src), reads=[Bk[bank_i]], writes=[Bdst], acc=True)

    xres_B = {}

    def Bx_of(job, t):
        return P.buf("xres_%d_%d" % (job, t))

    B_qd = [P.buf("q_d%d" % h) for h in range(4)]
    B_kd = [P.buf("k_d%d" % h) for h in range(4)]
    B_vd, B_ud = P.buf("v_d"), P.buf("u_d")
    B_mix = [P.buf("mix_d%d" % i) for i in range(smax // 512)]

    WOUT_OFF = ARENA_N - 30 * 1024
    Wout = arena.ap[:, WOUT_OFF:WOUT_OFF + 8 * 1024].rearrange("p (a b) -> p a b", a=8)
    Wd = arena.ap[:, WOUT_OFF + 8 * 1024:ARENA_N].rearrange("p (a b) -> p a b", a=NJ)
    B_Wout, B_Wd = P.buf("fWout", dma=True), P.buf("fWd", dma=True)

    def load_F_weights(l):
        for kc in range(8):
            dma("pool", Wout[:, kc, :], w_out[l, kc * 128:(kc + 1) * 128, :], [], [B_Wout], B_Wout, acc=(kc > 0))
        for j in range(NJ):
            dma("pool", Wd[:, j, :], w_down[l, j * 128:(j + 1) * 128, :], [], [B_Wd], B_Wd, acc=(j > 0))

    def pass_P(job, l, xsrc, S):
        arena.reset()
        Win = arena.bf(8, 2048)
        B_Wg = [P.buf("Win%d" % i, dma=True) for i in range(3)]
        xts = [(arena.f32(D_MODEL), P.buf("pxt%d" % i, dma=True)) for i in range(4)]
        junk, Bj = arena.bf(D_MODEL), P.buf("pjunk")
        xns = [(arena.bf(D_MODEL), P.buf("pxn%d" % i)) for i in range(4)]
        xnTs = [(arena.bf(8, 512), P.buf("pxnT%d" % i)) for i in range(2)]
        cosb = [(arena.f32(512), P.buf("pcos%d" % i, dma=True)) for i in range(2)]
        sinb = [(arena.f32(512), P.buf("psin%d" % i, dma=True)) for i in range(2)]
        qf = [(arena.f32(512), P.buf("pqf%d" % i)) for i in range(4)]
        qsw = [(arena.f32(512), P.buf("pqsw%d" % i, dma=True)) for i in range(4)]
        qko = [(arena.bf(512), P.buf("pqko%d" % i, dma=True)) for i in range(4)]
        vuo = [(arena.bf(512), P.buf("pvuo%d" % i, dma=True)) for i in range(4)]
        for cg, (ca, cz) in enumerate(((0, 512), (512, 1024), (1024, 2048))):
            for kc in range(8):
                dma("pool", Win[:, kc, ca:cz], w_inx[l, kc * 128:(kc + 1) * 128, ca:cz], [], [B_Wg[cg]], B_Wg[cg], acc=(kc > 0))
        dma("sp", g1b, n1g[l].partition_broadcast(128), [], [B_g1b], B_g1b)
        if job == 0 and l == 0:
            prep_layer(0)
        nb = S // 512
        ti = 0
        qi = 0
        vi = 0
        pbank = [(2, 3), (4, 5), (6, 7)]
        pbi = 0
        sb_i = 2
        def xloads(b):
            for i in range(4):
                t = b * 4 + i
                xt, Bxt = xts[t % 4]
                dma("sp", xt, xsrc[t * 128:(t + 1) * 128, :], [Bx_of(job, t)] if l > 0 else [], [Bxt], Bxt)

        def norm1(b, i):
            t = b * 4 + i
            xt, Bxt = xts[t % 4]
            xn, Bxn = xns[t % 4]
            norm_tile(xt, Bxt, g1b, B_g1b, xn, Bxn, junk, Bj)

        def norms(b):
            xloads(b)
            for i in range(4):
                norm1(b, i)

        def transposes(b):
            xnT, BxnT = xnTs[b % 2]
            for i in range(4):
                t = b * 4 + i
                xn, Bxn = xns[t % 4]
                transpose_tile(xn, Bxn, t % 2, xnT[:, :, i * 128:(i + 1) * 128], BxnT, "act" if i % 2 == 0 else "dve")

        norms(0)
        transposes(0)
        for b in range(nb):
            (cb, Bcb), (sbk, Bsb) = cosb[b % 2], sinb[b % 2]
            dma("sp", cb, rcos[:, b * 512:(b + 1) * 512], [], [Bcb], Bcb)
            dma("sp", sbk, rsin[:, b * 512:(b + 1) * 512], [], [Bsb], Bsb)
            xnT, BxnT = xnTs[b % 2]
            if b + 1 < nb:
                xloads(b + 1)
            late = []
            for which, (c0, Bd, dd) in enumerate([(0, B_qd, q_d), (512, B_kd, k_d)]):
                B_Win = B_Wg[which]
                for h in range(4):
                    bk = 2 + (sb_i % 6)
                    sb_i += 1
                    col0 = c0 + h * 128
                    for kc in range(8):
                        P.op("pe", lambda e, bk=bk, col0=col0, kc=kc, xnT=xnT: e.matmul(banks[bk], lhsT=Win[:, kc, col0:col0 + 128], rhs=xnT[:, kc, :],
                                                                              start=(kc == 0), stop=(kc == 7)),
                             reads=[B_Win, BxnT], writes=[Bk[bk]], acc=True)
                    (f, Bf), (w, Bw) = qf[qi % 4], qsw[qi % 4]
                    qo, Bqo = qko[qi % 4]
                    qi += 1
                    P.op("act", lambda e, bk=bk, f=f: e.activation(out=f, in_=banks[bk], func=AF.Copy), reads=[Bk[bk]], writes=[Bf])
                    for n_, a0 in enumerate((0, 32, 64, 96)):
                        dma("sp", w[a0:a0 + 32, :], f[(a0 ^ 32):(a0 ^ 32) + 32, :], [Bf], [Bw], Bw, acc=(n_ > 0))

                    def fin(f=f, Bf=Bf, w=w, Bw=Bw, qo=qo, Bqo=Bqo, cb=cb, Bcb=Bcb, sbk=sbk, Bsb=Bsb, dst=dd[h, :, b * 512:(b + 1) * 512], Bdst=Bd[h]):
                        P.op("dve", lambda e: e.tensor_tensor(out=f, in0=f, in1=cb, op=ALU.mult), reads=[Bf, Bcb], writes=[Bf])
                        P.op("dve", lambda e: e.tensor_tensor(out=w, in0=w, in1=sbk, op=ALU.mult), reads=[Bw, Bsb], writes=[Bw])
                        P.op("pool", lambda e: e.tensor_tensor(out=qo, in0=f, in1=w, op=ALU.add), reads=[Bf, Bw], writes=[Bqo])
                        dma("pool", dst, qo, [Bqo], [Bdst], Bqo, acc=True)

                    late.append(fin)
                    if len(late) > 1:
                        late.pop(0)()
                    if b + 1 < nb and (4 * which + h) % 2 == 1:
                        norm1(b + 1, (4 * which + h) // 2)
            while late:
                late.pop(0)()
            B_Win = B_Wg[2]
            for (c0, Bd, dd) in ((1024, B_vd, v_d), (1536, B_ud, u_d)):
                for i in range(4):
                    t = b * 4 + i
                    bk = 2 + (sb_i % 6)
                    sb_i += 1
                    for kc in range(8):
                        P.op("pe", lambda e, bk=bk, kc=kc, i=i, c0=c0, xnT=xnT: e.matmul(banks[bk], lhsT=xnT[:, kc, i * 128:(i + 1) * 128], rhs=Win[:, kc, c0:c0 + 512],
                                                                              start=(kc == 0), stop=(kc == 7)),
                             reads=[B_Win, BxnT], writes=[Bk[bk]], acc=True)
                    vo, Bvo = vuo[vi % 4]
                    vi += 1
                    if vi % 2 == 0:
                        P.op("act", lambda e, bk=bk, vo=vo: e.activation(out=vo, in_=banks[bk], func=AF.Copy), reads=[Bk[bk]], writes=[Bvo])
                    else:
                        P.op("dve", lambda e, bk=bk, vo=vo: e.tensor_copy(out=vo, in_=banks[bk]), reads=[Bk[bk]], writes=[Bvo])
                    dma("sp", dd[t * 128:(t + 1) * 128, :], vo, [Bvo], [Bd], Bvo, acc=True)
            if b + 1 < nb:
                transposes(b + 1)
        P.barrier()

    def pass_D(job, l, S):
        arena.reset()
        NT = S // 128
        HT = NT // 2
        NMB = (S // 2) // 512
        G = 2
        u_sb, B_u = arena.bf(NT, 512), P.buf("du_sb", dma=True)
        ueo = [(arena.bf(HT, 512), P.buf("due%d" % i)) for i in range(2)]
        ring = [(arena.bf(G, 512), P.buf("dring%d" % i, dma=True)) for i in range(6)]
        Y = [(arena.bf(4, 512), P.buf("dY%d" % i)) for i in range(2)]
        zo = [(arena.bf(4, 1024), P.buf("dzo%d" % i, dma=True)) for i in range(2)]
        assert arena.off <= WOUT_OFF, arena.off
        nld = 4
        for i in range(nld):
            t0, t1 = i * NT // nld, (i + 1) * NT // nld
            dma("sp", u_sb[:, t0:t1, :], u_d[t0 * 128:t1 * 128, :].rearrange("(t p) c -> p t c", p=128), [B_ud], [B_u], B_u, acc=(i > 0))
        CH = 2
        for i in range(0, HT, CH):
            for par, (eng, op_) in enumerate((("dve", ALU.add), ("pool", ALU.subtract))):
                dst, Bdst = ueo[par]
                P.op(eng, lambda e, i=i, dst=dst, op_=op_: e.tensor_tensor(out=dst[:, i:i + CH, :], in0=u_sb[:, i:i + CH, :], in1=u_sb[:, HT + i:HT + i + CH, :], op=op_),
                     reads=[B_u], writes=[Bdst], acc=(i > 0))
        ri = 0
        ev = 0
        tabs = dft[S]
        for mb in range(NMB):
            z, Bz = zo[mb % 2]
            z4 = z.rearrange("p g (m two) -> p g m two", two=2)
            for par in range(2):
                usrc, Busrc = ueo[par]
                for trig in range(2):
                    tab = tabs[2 * par + trig]
                    bset = [0, 1, 2, 3] if trig == 0 else [4, 5, 6, 7]
                    for tt in range(HT):
                        if tt % G == 0:
                            rg, Brg = ring[ri % 6]
                            ri += 1
                            dma("sp", rg, tab[tt * 128:(tt + G) * 128, mb * 512:(mb + 1) * 512].rearrange("(g p) n -> p g n", p=128), [], [Brg], Brg)
                        for g in range(4):
                            P.op("pe", lambda e, g=g, tt=tt, rg=rg, bset=bset, usrc=usrc: e.matmul(banks[bset[g]], lhsT=usrc[:, tt, g * 128:(g + 1) * 128], rhs=rg[:, tt % G, :],
                                                                                              start=(tt == 0), stop=(tt == HT - 1)),
                                 reads=[Busrc, Brg], writes=[Bk[bset[g]]], acc=True)
                    Yt, BY = Y[trig]
                    for g in range(4):
                        ev += 1
                        if ev % 2 == 0:
                            P.op("act", lambda e, g=g, Yt=Yt, bset=bset: e.activation(out=Yt[:, g, :], in_=banks[bset[g]], func=AF.Copy), reads=[Bk[bset[g]]], writes=[BY], acc=(g > 0))
                        else:
                            P.op("dve", lambda e, g=g, Yt=Yt, bset=bset: e.tensor_copy(out=Yt[:, g, :], in_=banks[bset[g]]), reads=[Bk[bset[g]]], writes=[BY], acc=(g > 0))
                for g in range(4):
                    P.op("pe", lambda e, g=g: e.matmul(banks[g], lhsT=chc, rhs=Y[0][0][:, g, :], start=True, stop=False), reads=[B_chc, Y[0][1]], writes=[Bk[g]], acc=True)
                    P.op("pe", lambda e, g=g: e.matmul(banks[g], lhsT=chs, rhs=Y[1][0][:, g, :], start=False, stop=True), reads=[B_chs, Y[1][1]], writes=[Bk[g]], acc=True)
                    ev += 1
                    first = (par == 0 and g == 0)
                    if ev % 2 == 0:
                        P.op("act", lambda e, g=g, z4=z4, par=par: e.activation(out=z4[:, g, :, par], in_=banks[g], func=AF.Copy), reads=[Bk[g]], writes=[Bz], acc=(not first))
                    else:
                        P.op("dve", lambda e, g=g, z4=z4, par=par: e.tensor_copy(out=z4[:, g, :, par], in_=banks[g]), reads=[Bk[g]], writes=[Bz], acc=(not first))
            dma("sp", mix_d[4:8, :, mb * 1024:(mb + 1) * 1024].rearrange("c p n -> p c n"), z, [Bz], [B_mix[2 * mb], B_mix[2 * mb + 1]], Bz, acc=True)
        P.barrier()

    def pass_A(job, l, S):
        arena.reset()
        prefetch_F = True
        NT, NB = S // 128, S // 512
        qh = [(arena.bf(S), P.buf("aq%d" % i, dma=True)) for i in range(2)]
        kh = [(arena.bf(S), P.buf("ak%d" % i, dma=True)) for i in range(2)]
        vh = [(arena.bf(NT, 128), P.buf("av%d" % i, dma=True)) for i in range(2)]
        NE = 4
        E = [(arena.bf(2, 512), P.buf("aE%d" % i)) for i in range(NE)]
        Es = [[(arena.bf(512), P.buf("aEs%d_%d" % (c, i))) for i in range(2)] for c in range(2)]
        rz = [(arena.f32(1024), P.buf("arz%d" % i)) for i in range(2)]
        oo = [(arena.f32(512), P.buf("aoo%d" % i)) for i in range(2)]
        ot = [(arena.f32(512), P.buf("aot%d" % i)) for i in range(2)]
        rs = [(arena.f32(512), P.buf("ars%d" % i)) for i in range(2)]
        sq = [(arena.bf(512), P.buf("asq%d" % i)) for i in range(2)]
        on = [(arena.bf(512), P.buf("aon%d" % i, dma=True)) for i in range(2)]
        SP_ = [(0, 1), (2, 3)]
        UB, ZB = [4, 5], [6, 7]

        def load(h):
            s = h % 2
            dma("sp", qh[s][0], q_d[h, :, 0:S], [B_qd[h]], [qh[s][1]], qh[s][1])
            dma("sp", kh[s][0], k_d[h, :, 0:S], [B_kd[h]], [kh[s][1]], kh[s][1])
            dma("sp", vh[s][0], v_d[0:S, h * 128:(h + 1) * 128].rearrange("(t p) e -> p t e", p=128), [B_vd], [vh[s][1]], vh[s][1])

        load(0)
        assert arena.off <= WOUT_OFF, arena.off
        load_F_weights(l)
        pend = []
        it = 0
        for h in range(4):
            if h + 1 < 4:
                load(h + 1)
            s = h % 2
            (q, Bq), (k, Bkk), (v, Bv) = qh[s], kh[s], vh[s]
            for qb in range(NB):
                p = it % 2
                it += 1

                def S_ops(kt, q=q, k=k, Bq=Bq, Bkk=Bkk, qb=qb):
                    b0, b1 = SP_[kt % 2]
                    Et, BE = E[kt % NE]
                    for c, bk in ((0, b0), (1, b1)):
                        P.op("pe", lambda e, c=c, bk=bk: e.matmul(banks[bk], lhsT=k[64 * c:64 * c + 64, kt * 128:(kt + 1) * 128],
                                                               rhs=q[64 * c:64 * c + 64, qb * 512:(qb + 1) * 512], start=True, stop=True),
                             reads=[Bq, Bkk], writes=[Bk[bk]])
                    src = bank2[b0 // 2].rearrange("p (c n) -> p c n", c=2)
                    P.op("act", lambda e: e.activation(out=Et, in_=src, func=AF.Exp, scale=0.125), reads=[Bk[b0], Bk[b1]], writes=[BE])

                def PV_ops(kt, v=v, Bv=Bv):
                    Et, BE = E[kt % NE]
                    for c in range(2):
                        P.op("pe", lambda e, c=c: e.matmul(banks[UB[c]], lhsT=v[:, kt, :], rhs=Et[:, c, :], start=(kt == 0), stop=(kt == NT - 1)),
                             reads=[Bv, BE], writes=[Bk[UB[c]]], acc=True)
                    if kt % 2 == 1:
                        Ep, BEp = E[(kt - 1) % NE]
                        for c, eng in ((0, "dve"), (1, "pool")):
                            es, Bes = Es[c][(kt // 2) % 2]
                            P.op(eng, lambda e, c=c, es=es: e.tensor_tensor(out=es, in0=Ep[:, c, :], in1=Et[:, c, :], op=ALU.add), reads=[BEp, BE], writes=[Bes])

                def Z_ops(j):
                    for c in range(2):
                        es, Bes = Es[c][j % 2]
                        P.op("pe", lambda e, c=c, es=es: e.matmul(banks[ZB[c]], lhsT=ones, rhs=es, start=(j == 0), stop=(j == NT // 2 - 1)),
                             reads=[B_ones, Bes], writes=[Bk[ZB[c]]], acc=True)

                S_ops(0)
                if NT > 1:
                    S_ops(1)
                for kt in range(NT):
                    if kt + 2 < NT:
                        S_ops(kt + 2)
                    PV_ops(kt)
                    if kt % 2 == 0 and kt >= 2:
                        Z_ops(kt // 2 - 1)
                    if kt == 3 and pend:
                        pend.pop(0)()
                Z_ops(NT // 2 - 1)
                (r01, Br01) = rz[p]
                (o0, Bo0), (o1, Bo1) = oo[0], oo[1]
                (o, Bo) = ot[p]
                (sqt, Bsq) = sq[p]
                zsrc = bank2[ZB[0] // 2]
                P.op("act", lambda e, r01=r01: e.activation(out=r01, in_=zsrc, func=AF.Ln), reads=[Bk[ZB[0]], Bk[ZB[1]]], writes=[Br01])
                P.op("act", lambda e, r01=r01: e.activation(out=r01, in_=r01, func=AF.Exp, scale=-1.0), reads=[Br01], writes=[Br01])
                r0, r1, Br0, Br1 = r01[:, 0:512], r01[:, 512:1024], Br01, Br01
                P.op("dve", lambda e, o0=o0, r0=r0: e.tensor_tensor(out=o0, in0=banks[UB[0]], in1=r0, op=ALU.mult), reads=[Bk[UB[0]], Br0], writes=[Bo0])
                P.op("dve", lambda e, o1=o1, r1=r1: e.tensor_tensor(out=o1, in0=banks[UB[1]], in1=r1, op=ALU.mult), reads=[Bk[UB[1]], Br1], writes=[Bo1])
                P.op("dve", lambda e, o=o, o0=o0, o1=o1: e.scalar_tensor_tensor(out=o, in0=o1, scalar=neglam[:, l:l + 1], in1=o0, op0=ALU.mult, op1=ALU.add),
                     reads=[Bo0, Bo1, B_lam], writes=[Bo])
                P.op("pool", lambda e, o=o, sqt=sqt: e.tensor_tensor(out=sqt, in0=o, in1=o, op=ALU.mult), reads=[Bo], writes=[Bsq])

                def part2(p=p, o=o, Bo=Bo, sqt=sqt, Bsq=Bsq, h=h, qb=qb):
                    nb_ = 0
                    (rst, Brs) = rs[p]
                    (ont, Bon) = on[p]
                    P.op("pe", lambda e: e.matmul(banks[nb_], lhsT=ones, rhs=sqt, start=True, stop=True), reads=[B_ones, Bsq], writes=[Bk[nb_]])
                    P.op("act", lambda e: e.activation(out=rst, in_=banks[nb_], func=AF.Ln, scale=1.0 / 128, bias=epsb[SUBLN_EPS]), reads=[Bk[nb_], B_eps], writes=[Brs])
                    P.op("act", lambda e: e.activation(out=rst, in_=rst, func=AF.Exp, scale=-0.5), reads=[Brs], writes=[Brs])
                    P.op("dve", lambda e: e.scalar_tensor_tensor(out=ont, in0=o, scalar=gs[:, l:l + 1], in1=rst, op0=ALU.mult, op1=ALU.mult),
                         reads=[Bo, Brs, B_lam], writes=[Bon])
                    dma("sp", mix_d[h, :, qb * 512:(qb + 1) * 512], ont, [Bon], [B_mix[qb]], Bon, acc=True)

                pend.append(part2)
        while pend:
            pend.pop(0)()
        P.barrier()

    def pass_F(job, l, xsrc, S, ydst, last):
        arena.reset()
        TB = min(1024, S)
        NTI = TB // 128
        NH = TB // 512
        NBLK = S // TB
        mixb = [(arena.bf(8, 256), P.buf("fmix%d" % i, dma=True)) for i in range(2)]
        xts = [(arena.f32(D_MODEL), P.buf("fxt%d" % i, dma=True)) for i in range(4)]
        x1s = [(arena.f32(D_MODEL), P.buf("fx1%d" % i, dma=True)) for i in range(2)]
        junk, Bj = arena.bf(D_MODEL), P.buf("fjunk")
        xns = [(arena.bf(D_MODEL), P.buf("fxn%d" % i)) for i in range(4)]
        xnT, BxnT = arena.bf(8, TB), P.buf("fxnT")
        hT, BhT = arena.bf(NJ, TB), P.buf("fhT")
        wgu = [(arena.bf(2, 8, 128), P.buf("fwgu%d" % i, dma=True)) for i in range(4)]
        sgt = [(arena.f32(512), P.buf("fsg%d" % i)) for i in range(2)]
        assert arena.off <= WOUT_OFF, arena.off
        dma("sp", g2b, n2g[l].partition_broadcast(128), [], [B_g2b], B_g2b)
        pairs = [(0, 1), (2, 3), (4, 5), (6, 7)]
        ctr = {"xt": 0, "x1": 0, "xn": 0, "w": 0, "p": 0, "t": 0, "s3": 0}

        def nxt(k, n):
            v = ctr[k] % n
            ctr[k] += 1
            return v

        def s1_front(blk, i):
            t = blk * NTI + i
            if i % 2 == 0:
                mb, Bmb = mixb[(t // 2) % 2]
                c0 = t * 128
                dma("sp", mb, mix_d[:, :, c0:c0 + 256].rearrange("c p n -> p c n"), [B_mix[t // 4]], [Bmb], Bmb)
            mb, Bmb = mixb[(t // 2) % 2]
            xt, Bxt = xts[nxt("xt", 4)]
            x1, Bx1 = x1s[nxt("x1", 2)]
            xn, Bxn = xns[nxt("xn", 4)]
            dma("sp", xt, xsrc[t * 128:(t + 1) * 128, :], [Bx_of(job, t)] if l > 0 else [], [Bxt], Bxt)
            for half, bk in ((0, 0), (1, 1)):
                for c in range(8):
                    P.op("pe", lambda e, bk=bk, c=c, half=half: e.matmul(banks[bk], lhsT=mb[:, c, (i % 2) * 128:(i % 2 + 1) * 128],
                                                                       rhs=Wout[:, c, half * 512:(half + 1) * 512], start=(c == 0), stop=(c == 7)),
                         reads=[Bmb, B_Wout], writes=[Bk[bk]], acc=True)
                P.op("dve", lambda e, bk=bk, half=half: e.tensor_tensor(out=x1[:, half * 512:(half + 1) * 512], in0=banks[bk],
                                                                      in1=xt[:, half * 512:(half + 1) * 512], op=ALU.add),
                     reads=[Bk[bk], Bxt], writes=[Bx1], acc=(half == 1))
            dma("sp", xres[jrow[job] + t * 128: jrow[job] + (t + 1) * 128, :], x1, [Bx1], [Bx_of(job, t)], Bx1)
            norm_tile(x1, Bx1, g2b, B_g2b, xn, Bxn, junk, Bj)
            return (xn, Bxn, i)

        def s1_back(item):
            xn, Bxn, i = item
            transpose_tile(xn, Bxn, 2 + nxt("t", 2), xnT[:, :, i * 128:(i + 1) * 128], BxnT, "act" if i % 2 == 0 else "dve")

        def stage2(blk):
            for j in range(NJ):
                wt, Bwt = wgu[nxt("w", 4)]
                dma("sp", wt, wgu_b[l, j], [B_wgu_d], [Bwt], Bwt)
                for half in range(NH):
                    bg, bu = pairs[nxt("p", 4)]
                    for (bk, gi) in ((bg, 0), (bu, 1)):
                        for kc in range(8):
                            P.op("pe", lambda e, bk=bk, gi=gi, kc=kc, wt=wt, half=half: e.matmul(banks[bk], lhsT=wt[:, gi, kc, :], rhs=xnT[:, kc, half * 512:(half + 1) * 512],
                                                                             start=(kc == 0), stop=(kc == 7)),
                                 reads=[Bwt, BxnT], writes=[Bk[bk]], acc=True)
                    sg, Bsg = sgt[ctr["p"] % 2]
                    P.op("act", lambda e, bg=bg, sg=sg: e.activation(out=sg, in_=banks[bg], func=AF.Silu), reads=[Bk[bg]], writes=[Bsg])
                    P.op("dve", lambda e, bu=bu, sg=sg, j=j, half=half: e.tensor_tensor(out=hT[:, j, half * 512:(half + 1) * 512], in0=banks[bu], in1=sg, op=ALU.mult),
                         reads=[Bk[bu], Bsg], writes=[BhT], acc=True)

        def s3_tile(blk, i):
            t = blk * NTI + i
            xt, Bxt = xts[nxt("xt", 4)]
            x2, Bx2 = x1s[nxt("x1", 2)]
            dma("sp", xt, xres[jrow[job] + t * 128: jrow[job] + (t + 1) * 128, :], [Bx_of(job, t)], [Bxt], Bxt)
            ba, bb = pairs[2 + nxt("s3", 2)]
            for half, bk in ((0, ba), (1, bb)):
                for j in range(NJ):
                    P.op("pe", lambda e, bk=bk, j=j, half=half: e.matmul(banks[bk], lhsT=hT[:, j, i * 128:(i + 1) * 128], rhs=Wd[:, j, half * 512:(half + 1) * 512],
                                                                       start=(j == 0), stop=(j == NJ - 1)),
                         reads=[BhT, B_Wd], writes=[Bk[bk]], acc=True)
                P.op("dve", lambda e, bk=bk, half=half: e.tensor_tensor(out=x2[:, half * 512:(half + 1) * 512], in0=banks[bk],
                                                                      in1=xt[:, half * 512:(half + 1) * 512], op=ALU.add),
                     reads=[Bk[bk], Bxt], writes=[Bx2], acc=(half == 1))
            if not last:
                dma("sp", xres[jrow[job] + t * 128: jrow[job] + (t + 1) * 128, :], x2, [Bx2], [Bx_of(job, t)], Bx2)
            else:
                col, Bc = smallcol()
                P.op("act", lambda e: e.activation(out=junk, in_=x2, func=AF.Square, accum_out=col[:, 0:1]), reads=[Bx2], writes=[Bj, Bc])
                P.op("act", lambda e: e.activation(out=col[:, 1:2], in_=col[:, 0:1], func=AF.Ln, scale=1.0 / D_MODEL, bias=epsb[NORM_EPS]),
                     reads=[Bc, B_eps], writes=[Bc])
                P.op("act", lambda e: e.activation(out=col[:, 2:3], in_=col[:, 1:2], func=AF.Exp, scale=-0.5), reads=[Bc], writes=[Bc])
                P.op("dve", lambda e: e.scalar_tensor_tensor(out=xt, in0=x2, scalar=col[:, 2:3], in1=gfb, op0=ALU.mult, op1=ALU.mult),
                     reads=[Bx2, Bc, B_gfb], writes=[Bxt])
                dma("sp", ydst[t * 128:(t + 1) * 128, :], xt, [Bxt], [], Bxt)

        LAG = 2
        backs = []
        for i in range(NTI):
            backs.append(s1_front(0, i))
            if len(backs) > LAG:
                s1_back(backs.pop(0))
        while backs:
            s1_back(backs.pop(0))
        if job == 0 and l + 1 < depth:
            prep_layer(l + 1)
        for blk in range(NBLK):
            stage2(blk)
            for i in range(NTI):
                s3_tile(blk, i)
                if blk + 1 < NBLK:
                    backs.append(s1_front(blk + 1, i))
                    if len(backs) > LAG:
                        s1_back(backs.pop(0))
            while backs:
                s1_back(backs.pop(0))
        P.barrier()

    jrow = []
    r = 0
    for (_, _, S) in jobspec:
        jrow.append(r)
        r += S
    for job, (nm, r0, S) in enumerate(jobspec):
        for l in range(depth):
            xsrc = xin[nm][r0:r0 + S, :] if l == 0 else xres[jrow[job]:jrow[job] + S, :]
            pass_P(job, l, xsrc, S)
            pass_D(job, l, S)
            pass_A(job, l, S)
            pass_F(job, l, xsrc, S, yout[nm][r0:r0 + S, :], l == depth - 1)
    P.emit()
    return nc


def _const_tables(smax, svals):
    bf = ml_dtypes.bfloat16
    inv = (1.0 / (10000.0 ** (np.arange(0, 64, 2, dtype=np.float32) / np.float32(64)))).astype(np.float32)
    pos = np.arange(smax, dtype=np.float32)
    p = np.arange(128)
    ang = (pos[None, :] * inv[p % 32][:, None]).astype(np.float32)
    rcos = np.cos(ang).astype(np.float32)
    sign = np.where((p % 64) < 32, -1.0, 1.0).astype(np.float32)
    rsin = (np.sin(ang) * sign[:, None]).astype(np.float32)
    c = np.arange(128)
    a = 2.0 * np.pi * ((c[:, None] * c[None, :]) % 128) / 128.0
    chc = (np.cos(a) / np.sqrt(128.0)).astype(bf)
    chs = (-np.sin(a) / np.sqrt(128.0)).astype(bf)
    out = {"rcos": rcos, "rsin": rsin, "chc": chc, "chs": chs}
    for S in svals:
        H = S // 2
        t = np.arange(H, dtype=np.int64)
        tabs = []
        for par in range(2):
            m = (t[:, None] * (2 * t[None, :] + par)) % S
            a = (2.0 * np.pi / S) * m.astype(np.float64)
            tabs.append((np.cos(a) / np.sqrt(S)).astype(np.float32).astype(bf))
            tabs.append((np.sin(a) / np.sqrt(S)).astype(np.float32).astype(bf))
        out["dft%d" % S] = np.stack(tabs, axis=0)
    return out


def _swap_cols():
    idx = []
    for n in range(8):
        idx += list(range(n * 64 + 32, n * 64 + 64)) + list(range(n * 64, n * 64 + 32))
    return np.array(idx)


def _shared_inputs(depth, norm1_g, w_in, lambda_q1, lambda_k1, lambda_q2, lambda_k2, subln_g, w_out, norm2_g,
                   w_gate, w_up, w_down, final_g, smax, svals):
    sw = _swap_cols()
    w_in = np.asarray(w_in, np.float32)
    w_inx = w_in
    lamv = np.stack([lambda_q1, lambda_k1, lambda_q2, lambda_k2], axis=1).astype(np.float32).reshape(-1)
    d = {
        "w_inx": np.ascontiguousarray(w_inx), "w_out": np.asarray(w_out, np.float32), "w_gate": np.asarray(w_gate, np.float32),
        "w_up": np.asarray(w_up, np.float32), "w_down": np.asarray(w_down, np.float32),
        "n1g": np.asarray(norm1_g, np.float32), "n2g": np.asarray(norm2_g, np.float32), "fg": np.asarray(final_g, np.float32),
        "lamv": np.ascontiguousarray(lamv), "sg": np.ascontiguousarray(np.asarray(subln_g, np.float32).T),
    }
    d.update(_const_tables(smax, svals))
    return d


_NC_CACHE = {}


def kernel(x_prompt, x_sample, norm1_g, w_in, lambda_q1, lambda_k1, lambda_q2, lambda_k2,
           subln_g, w_out, norm2_g, w_gate, w_up, w_down, final_g):
    x_prompt = np.asarray(x_prompt, np.float32)
    x_sample = np.asarray(x_sample, np.float32)
    depth = int(np.asarray(w_in).shape[0])
    n = 8
    Bp, Sp, _ = x_prompt.shape
    Bs, Ss, _ = x_sample.shape
    ppc, spc = Bp // n, Bs // n
    jobspec = [("xp", i * Sp, Sp) for i in range(ppc)] + [("xs", i * Ss, Ss) for i in range(spc)]
    smax = max(Sp, Ss)
    svals = sorted({Sp, Ss})
    key = (depth, tuple(jobspec))
    if key not in _NC_CACHE:
        _NC_CACHE[key] = build_program(depth, jobspec, smax)
    nc = _NC_CACHE[key]
    shared = _shared_inputs(depth, norm1_g, w_in, lambda_q1, lambda_k1, lambda_q2, lambda_k2, subln_g, w_out, norm2_g,
                            w_gate, w_up, w_down, final_g, smax, svals)
    in_maps = []
    for c in range(n):
        m = dict(shared)
        m["xp"] = np.ascontiguousarray(x_prompt[c * ppc:(c + 1) * ppc].reshape(ppc * Sp, D_MODEL))
        m["xs"] = np.ascontiguousarray(x_sample[c * spc:(c + 1) * spc].reshape(spc * Ss, D_MODEL))
        in_maps.append(m)
    res = run_bass_kernel_spmd(nc, in_maps, core_ids=list(range(n)))
    yp = np.concatenate([np.asarray(r["y_xp"]).reshape(ppc, Sp, D_MODEL) for r in res.results], axis=0)
    ys = np.concatenate([np.asarray(r["y_xs"]).reshape(spc, Ss, D_MODEL) for r in res.results], axis=0)
    return (yp.astype(np.float32), ys.astype(np.float32))
```

```python
import math
import numpy as np
import ml_dtypes
import concourse.bass as bass
import concourse.mybir as mybir
from concourse.bass_utils import run_bass_kernel_spmd

F32 = mybir.dt.float32
BF16 = mybir.dt.bfloat16
ALU = mybir.AluOpType
AF = mybir.ActivationFunctionType

D_MODEL = 1024
D_FF = 2816
NJ = D_FF // 128
NORM_EPS = 1e-6
SUBLN_EPS = 1e-5
BLOCKNAME = {"pe": "tensor", "act": "scalar", "dve": "vector", "pool": "gpsimd", "sp": "sync"}


class Buf:
    def __init__(self, prog, name, dma=False, excl=False):
        self.name = name
        self.excl = excl
        self.w = {}
        self.r = {}
        self.sem = None
        self.count = 0
        if dma:
            self.sem = prog.nc.semaphore("d_" + name).__enter__()
            prog.dma_bufs.append(self)


class Prog:
    ENG = ["pe", "act", "dve", "pool", "sp"]
    EPOCH = 30000

    def __init__(self, nc):
        self.nc = nc
        self.ops = {e: [] for e in self.ENG}
        self.sem = {e: nc.semaphore("e_" + e).__enter__() for e in ["pe", "act", "dve", "pool"]}
        self.cnt = {e: 0 for e in self.sem}
        self.epoch = {e: 0 for e in self.sem}
        self.last = {e: {} for e in self.sem}
        self.known = {e: {} for e in self.ENG}
        self.pending = {e: {} for e in self.ENG}
        self.dma_bufs = []
        self.bufs = {}

    def buf(self, name, dma=False, excl=False):
        if name not in self.bufs:
            self.bufs[name] = Buf(self, name, dma, excl)
        return self.bufs[name]

    def op(self, eng, fn, reads=(), writes=(), dma=None, acc=False):
        waits = dict(self.pending[eng])
        self.pending[eng] = {}

        def need(evs):
            for k, (s, v) in evs.items():
                if k not in waits or waits[k][1] < v:
                    waits[k] = (s, v)

        xr = [b for b in reads if b.excl]
        reads = [b for b in reads if not b.excl]
        writes = list(writes) + xr
        for b in reads:
            need(b.w)
        for b in writes:
            if b.excl:
                need(b.r)
                need(b.w)
            elif b.r:
                need(b.r)
            elif not acc:
                need(b.w)
        final = []
        for k, (s, v) in waits.items():
            if eng == "pe" and isinstance(k, tuple) and k[0] == "pe":
                continue
            if self.known[eng].get(k, 0) >= v:
                continue
            self.known[eng][k] = v
            final.append((s, v))
        if dma is None:
            if self.cnt[eng] >= self.EPOCH:
                self.epoch[eng] += 1
                self.cnt[eng] = 0
                self.sem[eng] = self.nc.semaphore("e_%s%d" % (eng, self.epoch[eng])).__enter__()
            self.cnt[eng] += 1
            key, sem, val, inc = (eng, self.epoch[eng]), self.sem[eng], self.cnt[eng], 1
            self.last[eng][key] = (sem, val)
        else:
            dma.count += 16
            key, sem, val, inc = id(dma), dma.sem, dma.count, 16
        for b in reads:
            b.r[key] = (sem, val)
        for b in writes:
            if b.excl:
                b.r = {}
                b.w = {key: (sem, val)}
                continue
            if b.r:
                b.r = {}
                b.w = {}
            elif not acc:
                b.w = {}
            b.w[key] = (sem, val)
        self.ops[eng].append((final, fn, sem, inc))

    def barrier(self):
        evs = {}
        for e in self.sem:
            evs.update(self.last[e])
        for b in self.dma_bufs:
            if b.count > 0:
                evs[id(b)] = (b.sem, b.count)
        for e in self.ENG:
            for k, (s, v) in evs.items():
                if k not in self.pending[e] or self.pending[e][k][1] < v:
                    self.pending[e][k] = (s, v)

    def emit(self):
        nc = self.nc
        self.barrier()
        fin = [(s, v) for k, (s, v) in self.pending["sp"].items() if self.known["sp"].get(k, 0) < v]
        with nc.Block() as block:
            for eng in self.ENG:
                ops = self.ops[eng]

                def body(e, ops=ops, eng=eng):
                    for waits, fn, sem, inc in ops:
                        for s, v in waits:
                            e.wait_ge(s, v)
                        fn(e).then_inc(sem, inc)
                    if eng == "sp":
                        for s, v in fin:
                            e.wait_ge(s, v)

                getattr(block, BLOCKNAME[eng])(body)


class Arena:
    def __init__(self, ap, n):
        self.ap, self.n, self.off = ap, n, 0

    def reset(self):
        self.off = 0

    def _raw(self, n):
        n = (n + 15) // 16 * 16
        a = self.off
        self.off += n
        assert self.off <= self.n, ("arena overflow", self.off, self.n)
        return self.ap[:, a:a + n]

    def bf(self, *shape):
        n = int(np.prod(shape))
        v = self._raw(n)[:, 0:n]
        if len(shape) == 2:
            v = v.rearrange("p (a b) -> p a b", a=shape[0])
        elif len(shape) == 3:
            v = v.rearrange("p (a b c) -> p a b c", a=shape[0], b=shape[1])
        return v

    def f32(self, *shape):
        n = int(np.prod(shape))
        v = self._raw(2 * n)[:, 0:2 * n].bitcast(F32)
        if len(shape) == 2:
            v = v.rearrange("p (a b) -> p a b", a=shape[0])
        return v


def lambda_init_fn(layer_idx):
    return 0.8 - 0.6 * math.exp(-0.3 * layer_idx)


def build_program(depth, jobspec, smax):
    nc = bass.Bass("TRN2", target_bir_lowering=False)
    P = Prog(nc)

    def din(name, shape, dt=F32):
        return nc.dram_tensor(name, list(shape), dt, kind="ExternalInput").ap()

    def dscr(name, shape, dt=BF16):
        return nc.dram_tensor(name, list(shape), dt, kind="Internal").ap()

    xin, yout, rows = {}, {}, {}
    for (nm, r0, S) in jobspec:
        rows[nm] = max(rows.get(nm, 0), r0 + S)
    for nm, r in rows.items():
        xin[nm] = din(nm, [r, D_MODEL])
        yout[nm] = nc.dram_tensor("y_" + nm, [r, D_MODEL], F32, kind="ExternalOutput").ap()
    w_inx = din("w_inx", [depth, D_MODEL, 2048])
    w_out = din("w_out", [depth, D_MODEL, D_MODEL])
    w_gate = din("w_gate", [depth, D_MODEL, D_FF])
    w_up = din("w_up", [depth, D_MODEL, D_FF])
    w_down = din("w_down", [depth, D_FF, D_MODEL])
    n1g = din("n1g", [depth, D_MODEL])
    n2g = din("n2g", [depth, D_MODEL])
    fg = din("fg", [D_MODEL])
    lamv_d = din("lamv", [depth * 4 * 64])
    sg_d = din("sg", [128, depth])
    rcos = din("rcos", [128, smax])
    rsin = din("rsin", [128, smax])
    chc_d = din("chc", [128, 128], BF16)
    chs_d = din("chs", [128, 128], BF16)
    svals = sorted(set(S for _, _, S in jobspec))
    dft = {S: din("dft%d" % S, [4, S // 2, S // 2], BF16) for S in svals}

    tot_rows = sum(S for _, _, S in jobspec)
    xres = dscr("xres", [tot_rows, D_MODEL], F32)
    q_d = dscr("q_d", [4, 128, smax])
    k_d = dscr("k_d", [4, 128, smax])
    v_d = dscr("v_d", [smax, 512])
    u_d = dscr("u_d", [smax, 512])
    mix_d = dscr("mix_d", [8, 128, smax])
    wgu_b = dscr("wgu_b", [depth, NJ, 128, 2, 8, 128])

    def sb(name, shape, dt):
        return nc.alloc_sbuf_tensor(name, list(shape), dt).ap()

    ident = sb("ident", [128, 128], BF16)
    identf = sb("identf", [128, 128], F32)
    ones = sb("ones", [128, 128], BF16)
    onesf = sb("onesf", [128, 128], F32)
    chc = sb("chc_s", [128, 128], BF16)
    chs = sb("chs_s", [128, 128], BF16)
    g1b = sb("g1b", [128, D_MODEL], F32)
    g2b = sb("g2b", [128, D_MODEL], F32)
    gfb = sb("gfb", [128, D_MODEL], F32)
    lamv = sb("lamv_s", [128, depth * 4 * 64], F32)
    lamt = sb("lamt", [128, depth * 4 * 64 // 2], F32)
    lams = sb("lams", [128, depth * 2], F32)
    neglam = sb("neglam", [128, depth], F32)
    gs = sb("gs", [128, depth], F32)
    sgv = sb("sgv", [128, depth], F32)
    small = sb("small", [128, 64], F32)
    ARENA_N = 92 * 1024
    arena = Arena(sb("arena", [128, ARENA_N], BF16), ARENA_N)
    bank2 = [nc.alloc_psum_tensor("bank2_%d" % i, [128, 1024], F32).ap() for i in range(4)]
    banks = [bank2[i // 2][:, (i % 2) * 512:(i % 2 + 1) * 512] for i in range(8)]
    Bk = [P.buf("bank%d" % i, excl=True) for i in range(8)]

    B_ident, B_ones = P.buf("ident"), P.buf("ones")
    B_chc, B_chs = P.buf("chc", dma=True), P.buf("chs", dma=True)
    B_g1b, B_g2b, B_gfb = P.buf("g1b", dma=True), P.buf("g2b", dma=True), P.buf("gfb", dma=True)
    B_lam, B_sgv = P.buf("lamv", dma=True), P.buf("sgv", dma=True)
    B_small = [P.buf("small%d" % i) for i in range(16)]
    small_ctr = [0]

    def smallcol():
        i = small_ctr[0] % 16
        small_ctr[0] += 1
        return small[:, 4 * i:4 * i + 4], B_small[i]

    def dma(eng, out, in_, reads, writes, sem, acc=False):
        P.op(eng, lambda e: e.dma_start(out=out, in_=in_), reads=reads, writes=writes, dma=sem, acc=acc)

    P.op("pool", lambda e: e.memset(identf, 0.0), writes=[B_ident])
    P.op("pool", lambda e: e.affine_select(out=identf, in_=identf, pattern=[[-1, 128]], compare_op=ALU.not_equal,
                                           fill=1.0, base=0, channel_multiplier=1), reads=[B_ident], writes=[B_ident])
    P.op("dve", lambda e: e.tensor_copy(out=ident, in_=identf), reads=[B_ident], writes=[B_ident])
    P.op("dve", lambda e: e.memset(ones, 1.0), writes=[B_ones])
    P.op("dve", lambda e: e.memset(onesf, 1.0), writes=[B_ones], acc=True)
    dma("sp", chc, chc_d, [], [B_chc], B_chc)
    dma("sp", chs, chs_d, [], [B_chs], B_chs)
    dma("sp", gfb, fg.partition_broadcast(128), [], [B_gfb], B_gfb)
    dma("sp", lamv, lamv_d.partition_broadcast(128), [], [B_lam], B_lam)
    dma("sp", sgv, sg_d, [], [B_sgv], B_sgv)
    lv = lamv.rearrange("p (l f d) -> p l f d", l=depth, f=4)
    lt = lamt.rearrange("p (l f d) -> p l f d", l=depth, f=2)
    for l in range(depth):
        for f in range(2):
            P.op("dve", lambda e, l=l, f=f: e.tensor_tensor(out=lt[:, l, f, :], in0=lv[:, l, 2 * f, :], in1=lv[:, l, 2 * f + 1, :],
                                                            op=ALU.mult), reads=[B_lam], writes=[B_lam])
            P.op("dve", lambda e, l=l, f=f: e.reduce_sum(out=lams[:, 2 * l + f:2 * l + f + 1], in_=lt[:, l, f, :],
                                                         axis=mybir.AxisListType.X), reads=[B_lam], writes=[B_lam])
    P.op("act", lambda e: e.activation(out=lams, in_=lams, func=AF.Exp), reads=[B_lam], writes=[B_lam])
    for l in range(depth):
        li = lambda_init_fn(l)
        P.op("dve", lambda e, l=l, li=li: e.scalar_tensor_tensor(out=neglam[:, l:l + 1], in0=lams[:, 2 * l + 1:2 * l + 2], scalar=-li,
                                                                 in1=lams[:, 2 * l:2 * l + 1], op0=ALU.add, op1=ALU.subtract),
             reads=[B_lam], writes=[B_lam])
        P.op("dve", lambda e, l=l, li=li: e.tensor_scalar(out=gs[:, l:l + 1], in0=sgv[:, l:l + 1], scalar1=1.0 - li, scalar2=None,
                                                          op0=ALU.mult), reads=[B_sgv, B_lam], writes=[B_lam])

    B_wgu_d = P.buf("wgu_d")
    B_prep = P.buf("prep", dma=True)

    def prep_layer(l):
        for j in range(NJ):
            dma("pool", wgu_b[l, j, :, 0, :, :], w_gate[l, :, j * 128:(j + 1) * 128].rearrange("(kc p) n -> p kc n", p=128), [], [B_wgu_d], B_prep, acc=True)
            dma("pool", wgu_b[l, j, :, 1, :, :], w_up[l, :, j * 128:(j + 1) * 128].rearrange("(kc p) n -> p kc n", p=128), [], [B_wgu_d], B_prep, acc=True)

    def rstd_from(ss_ap, Bss, scale, eps):
        col, Bc = smallcol()
        P.op("act", lambda e: e.activation(out=col[:, 0:1], in_=ss_ap, func=AF.Ln, scale=scale, bias=epsb[eps]), reads=[Bss, B_eps], writes=[Bc])
        P.op("act", lambda e: e.activation(out=col[:, 1:2], in_=col[:, 0:1], func=AF.Exp, scale=-0.5), reads=[Bc], writes=[Bc])
        return col[:, 1:2], Bc

    epst = sb("epst", [128, 2], F32)
    B_eps = P.buf("eps")
    P.op("dve", lambda e: e.memset(epst[:, 0:1], NORM_EPS), writes=[B_eps])
    P.op("dve", lambda e: e.memset(epst[:, 1:2], SUBLN_EPS), writes=[B_eps], acc=True)
    epsb = {NORM_EPS: epst[:, 0:1], SUBLN_EPS: epst[:, 1:2]}

    def norm_tile(xt, Bx, gb, Bg, xn, Bxn, junk, Bj):
        col, Bc = smallcol()
        P.op("act", lambda e: e.activation(out=junk, in_=xt, func=AF.Square, accum_out=col[:, 0:1]), reads=[Bx], writes=[Bj, Bc])
        P.op("act", lambda e: e.activation(out=col[:, 1:2], in_=col[:, 0:1], func=AF.Ln, scale=1.0 / D_MODEL, bias=epsb[NORM_EPS]),
             reads=[Bc, B_eps], writes=[Bc])
        P.op("act", lambda e: e.activation(out=col[:, 2:3], in_=col[:, 1:2], func=AF.Exp, scale=-0.5), reads=[Bc], writes=[Bc])
        P.op("dve", lambda e: e.scalar_tensor_tensor(out=xn, in0=xt, scalar=col[:, 2:3], in1=gb, op0=ALU.mult, op1=ALU.mult),
             reads=[Bx, Bc, Bg], writes=[Bxn])

    def transpose_tile(xn, Bxn, bank_i, dst, Bdst, eng):
        psT = banks[bank_i].bitcast(BF16)
        for c in range(8):
            P.op("pe", lambda e, c=c: e.transpose(out=psT[:, c * 128:(c + 1) * 128], in_=xn[:, c * 128:(c + 1) * 128], identity=ident),
                 reads=[Bxn, B_ident], writes=[Bk[bank_i]], acc=True)
        src = psT.rearrange("p (c n) -> p c n", c=8)
        if eng == "act":
            P.op("act", lambda e: e.activation(out=dst, in_=src, func=AF.Copy), reads=[Bk[bank_i]], writes=[Bdst], acc=True)
        else:
            P.op("dve", lambda e: e.tensor_copy(out=dst, in_=src), reads=[Bk[bank_i]], writes=[Bdst], acc=True)

    xres_B = {}

    def Bx_of(job, t):
        return P.buf("xres_%d_%d" % (job, t))

    B_qd = [P.buf("q_d%d" % h) for h in range(4)]
    B_kd = [P.buf("k_d%d" % h) for h in range(4)]
    B_vd, B_ud = P.buf("v_d"), P.buf("u_d")
    B_mix = [P.buf("mix_d%d" % i) for i in range(smax // 512)]

    WOUT_OFF = ARENA_N - 30 * 1024
    Wout = arena.ap[:, WOUT_OFF:WOUT_OFF + 8 * 1024].rearrange("p (a b) -> p a b", a=8)
    Wd = arena.ap[:, WOUT_OFF + 8 * 1024:ARENA_N].rearrange("p (a b) -> p a b", a=NJ)
    B_Wout, B_Wd = P.buf("fWout", dma=True), P.buf("fWd", dma=True)

    def load_F_weights(l):
        for kc in range(8):
            dma("pool", Wout[:, kc, :], w_out[l, kc * 128:(kc + 1) * 128, :], [], [B_Wout], B_Wout, acc=(kc > 0))
        for j in range(NJ):
            dma("pool", Wd[:, j, :], w_down[l, j * 128:(j + 1) * 128, :], [], [B_Wd], B_Wd, acc=(j > 0))

    def pass_P(job, l, xsrc, S):
        arena.reset()
        Win = arena.bf(8, 2048)
        B_Wg = [P.buf("Win%d" % i, dma=True) for i in range(3)]
        xts = [(arena.f32(D_MODEL), P.buf("pxt%d" % i, dma=True)) for i in range(4)]
        junk, Bj = arena.bf(D_MODEL), P.buf("pjunk")
        xns = [(arena.bf(D_MODEL), P.buf("pxn%d" % i)) for i in range(4)]
        xnTs = [(arena.bf(8, 512), P.buf("pxnT%d" % i)) for i in range(2)]
        cosb = [(arena.f32(512), P.buf("pcos%d" % i, dma=True)) for i in range(2)]
        sinb = [(arena.f32(512), P.buf("psin%d" % i, dma=True)) for i in range(2)]
        qf = [(arena.f32(512), P.buf("pqf%d" % i)) for i in range(4)]
        qsw = [(arena.f32(512), P.buf("pqsw%d" % i, dma=True)) for i in range(4)]
        qko = [(arena.bf(512), P.buf("pqko%d" % i, dma=True)) for i in range(4)]
        vuo = [(arena.bf(512), P.buf("pvuo%d" % i, dma=True)) for i in range(4)]
        for cg, (ca, cz) in enumerate(((0, 512), (512, 1024), (1024, 2048))):
            for kc in range(8):
                dma("pool", Win[:, kc, ca:cz], w_inx[l, kc * 128:(kc + 1) * 128, ca:cz], [], [B_Wg[cg]], B_Wg[cg], acc=(kc > 0))
        dma("sp", g1b, n1g[l].partition_broadcast(128), [], [B_g1b], B_g1b)
        if job == 0 and l == 0:
            prep_layer(0)
        nb = S // 512
        ti = 0
        qi = 0
        vi = 0
        pbank = [(2, 3), (4, 5), (6, 7)]
        pbi = 0
        sb_i = 2
        def xloads(b):
            for i in range(4):
                t = b * 4 + i
                xt, Bxt = xts[t % 4]
                dma("sp", xt, xsrc[t * 128:(t + 1) * 128, :], [Bx_of(job, t)] if l > 0 else [], [Bxt], Bxt)

        def norm1(b, i):
            t = b * 4 + i
            xt, Bxt = xts[t % 4]
            xn, Bxn = xns[t % 4]
            norm_tile(xt, Bxt, g1b, B_g1b, xn, Bxn, junk, Bj)

        def norms(b):
            xloads(b)
            for i in range(4):
                norm1(b, i)

        def transposes(b):
            xnT, BxnT = xnTs[b % 2]
            for i in range(4):
                t = b * 4 + i
                xn, Bxn = xns[t % 4]
                transpose_tile(xn, Bxn, t % 2, xnT[:, :, i * 128:(i + 1) * 128], BxnT, "act")

        norms(0)
        transposes(0)
        for b in range(nb):
            (cb, Bcb), (sbk, Bsb) = cosb[b % 2], sinb[b % 2]
            dma("sp", cb, rcos[:, b * 512:(b + 1) * 512], [], [Bcb], Bcb)
            dma("sp", sbk, rsin[:, b * 512:(b + 1) * 512], [], [Bsb], Bsb)
            xnT, BxnT = xnTs[b % 2]
            if b + 1 < nb:
                xloads(b + 1)
            late = []
            for which, (c0, Bd, dd) in enumerate([(0, B_qd, q_d), (512, B_kd, k_d)]):
                B_Win = B_Wg[which]
                for h in range(4):
                    bk = 2 + (sb_i % 6)
                    sb_i += 1
                    col0 = c0 + h * 128
                    for kc in range(8):
                        P.op("pe", lambda e, bk=bk, col0=col0, kc=kc, xnT=xnT: e.matmul(banks[bk], lhsT=Win[:, kc, col0:col0 + 128], rhs=xnT[:, kc, :],
                                                                              start=(kc == 0), stop=(kc == 7)),
                             reads=[B_Win, BxnT], writes=[Bk[bk]], acc=True)
                    (f, Bf), (w, Bw) = qf[qi % 4], qsw[qi % 4]
                    qo, Bqo = qko[qi % 4]
                    qi += 1
                    P.op("act", lambda e, bk=bk, f=f: e.activation(out=f, in_=banks[bk], func=AF.Copy), reads=[Bk[bk]], writes=[Bf])
                    for n_, a0 in enumerate((0, 32, 64, 96)):
                        dma("sp", w[a0:a0 + 32, :], f[(a0 ^ 32):(a0 ^ 32) + 32, :], [Bf], [Bw], Bw, acc=(n_ > 0))

                    def fin(f=f, Bf=Bf, w=w, Bw=Bw, qo=qo, Bqo=Bqo, cb=cb, Bcb=Bcb, sbk=sbk, Bsb=Bsb, dst=dd[h, :, b * 512:(b + 1) * 512], Bdst=Bd[h]):
                        P.op("dve", lambda e: e.tensor_tensor(out=f, in0=f, in1=cb, op=ALU.mult), reads=[Bf, Bcb], writes=[Bf])
                        P.op("dve", lambda e: e.tensor_tensor(out=w, in0=w, in1=sbk, op=ALU.mult), reads=[Bw, Bsb], writes=[Bw])
                        P.op("pool", lambda e: e.tensor_tensor(out=qo, in0=f, in1=w, op=ALU.add), reads=[Bf, Bw], writes=[Bqo])
                        dma("pool", dst, qo, [Bqo], [Bdst], Bqo, acc=True)

                    late.append(fin)
                    if len(late) > 1:
                        late.pop(0)()
                    if b + 1 < nb and (4 * which + h) % 2 == 1:
                        norm1(b + 1, (4 * which + h) // 2)
            while late:
                late.pop(0)()
            B_Win = B_Wg[2]
            for (c0, Bd, dd) in ((1024, B_vd, v_d), (1536, B_ud, u_d)):
                for i in range(4):
                    t = b * 4 + i
                    bk = 2 + (sb_i % 6)
                    sb_i += 1
                    for kc in range(8):
                        P.op("pe", lambda e, bk=bk, kc=kc, i=i, c0=c0, xnT=xnT: e.matmul(banks[bk], lhsT=xnT[:, kc, i * 128:(i + 1) * 128], rhs=Win[:, kc, c0:c0 + 512],
                                                                              start=(kc == 0), stop=(kc == 7)),
                             reads=[B_Win, BxnT], writes=[Bk[bk]], acc=True)
                    vo, Bvo = vuo[vi % 4]
                    vi += 1
                    P.op("act", lambda e, bk=bk, vo=vo: e.activation(out=vo, in_=banks[bk], func=AF.Copy), reads=[Bk[bk]], writes=[Bvo])
                    dma("sp", dd[t * 128:(t + 1) * 128, :], vo, [Bvo], [Bd], Bvo, acc=True)
            if b + 1 < nb:
                transposes(b + 1)
        P.barrier()

    def pass_D(job, l, S):
        arena.reset()
        NT = S // 128
        HT = NT // 2
        NMB = (S // 2) // 512
        G = 2
        u_sb, B_u = arena.bf(NT, 512), P.buf("du_sb", dma=True)
        ueo = [(arena.bf(HT, 512), P.buf("due%d" % i)) for i in range(2)]
        ring = [(arena.bf(G, 512), P.buf("dring%d" % i, dma=True)) for i in range(6)]
        Y = [(arena.bf(4, 512), P.buf("dY%d" % i)) for i in range(2)]
        zo = [(arena.bf(4, 1024), P.buf("dzo%d" % i, dma=True)) for i in range(2)]
        assert arena.off <= WOUT_OFF, arena.off
        nld = 4
        for i in range(nld):
            t0, t1 = i * NT // nld, (i + 1) * NT // nld
            dma("sp", u_sb[:, t0:t1, :], u_d[t0 * 128:t1 * 128, :].rearrange("(t p) c -> p t c", p=128), [B_ud], [B_u], B_u, acc=(i > 0))
        CH = 2
        for i in range(0, HT, CH):
            for par, (eng, op_) in enumerate((("dve", ALU.add), ("pool", ALU.subtract))):
                dst, Bdst = ueo[par]
                P.op(eng, lambda e, i=i, dst=dst, op_=op_: e.tensor_tensor(out=dst[:, i:i + CH, :], in0=u_sb[:, i:i + CH, :], in1=u_sb[:, HT + i:HT + i + CH, :], op=op_),
                     reads=[B_u], writes=[Bdst], acc=(i > 0))
        ri = 0
        ev = 0
        tabs = dft[S]
        for mb in range(NMB):
            z, Bz = zo[mb % 2]
            z4 = z.rearrange("p g (m two) -> p g m two", two=2)
            for par in range(2):
                usrc, Busrc = ueo[par]
                for trig in range(2):
                    tab = tabs[2 * par + trig]
                    bset = [0, 1, 2, 3] if trig == 0 else [4, 5, 6, 7]
                    for tt in range(HT):
                        if tt % G == 0:
                            rg, Brg = ring[ri % 6]
                            ri += 1
                            dma("sp", rg, tab[tt * 128:(tt + G) * 128, mb * 512:(mb + 1) * 512].rearrange("(g p) n -> p g n", p=128), [], [Brg], Brg)
                        for g in range(4):
                            P.op("pe", lambda e, g=g, tt=tt, rg=rg, bset=bset, usrc=usrc: e.matmul(banks[bset[g]], lhsT=usrc[:, tt, g * 128:(g + 1) * 128], rhs=rg[:, tt % G, :],
                                                                                              start=(tt == 0), stop=(tt == HT - 1)),
                                 reads=[Busrc, Brg], writes=[Bk[bset[g]]], acc=True)
                    Yt, BY = Y[trig]
                    for g in range(4):
                        ev += 1
                        if ev % 2 == 0:
                            P.op("act", lambda e, g=g, Yt=Yt, bset=bset: e.activation(out=Yt[:, g, :], in_=banks[bset[g]], func=AF.Copy), reads=[Bk[bset[g]]], writes=[BY], acc=(g > 0))
                        else:
                            P.op("dve", lambda e, g=g, Yt=Yt, bset=bset: e.tensor_copy(out=Yt[:, g, :], in_=banks[bset[g]]), reads=[Bk[bset[g]]], writes=[BY], acc=(g > 0))
                for g in range(4):
                    P.op("pe", lambda e, g=g: e.matmul(banks[g], lhsT=chc, rhs=Y[0][0][:, g, :], start=True, stop=False), reads=[B_chc, Y[0][1]], writes=[Bk[g]], acc=True)
                    P.op("pe", lambda e, g=g: e.matmul(banks[g], lhsT=chs, rhs=Y[1][0][:, g, :], start=False, stop=True), reads=[B_chs, Y[1][1]], writes=[Bk[g]], acc=True)
                    ev += 1
                    first = (par == 0 and g == 0)
                    if ev % 2 == 0:
                        P.op("act", lambda e, g=g, z4=z4, par=par: e.activation(out=z4[:, g, :, par], in_=banks[g], func=AF.Copy), reads=[Bk[g]], writes=[Bz], acc=(not first))
                    else:
                        P.op("dve", lambda e, g=g, z4=z4, par=par: e.tensor_copy(out=z4[:, g, :, par], in_=banks[g]), reads=[Bk[g]], writes=[Bz], acc=(not first))
            dma("sp", mix_d[4:8, :, mb * 1024:(mb + 1) * 1024].rearrange("c p n -> p c n"), z, [Bz], [B_mix[2 * mb], B_mix[2 * mb + 1]], Bz, acc=True)
        P.barrier()

    def pass_A(job, l, S):
        arena.reset()
        prefetch_F = True
        NT, NB = S // 128, S // 512
        qh = [(arena.bf(S), P.buf("aq%d" % i, dma=True)) for i in range(2)]
        kh = [(arena.bf(S), P.buf("ak%d" % i, dma=True)) for i in range(2)]
        vh = [(arena.bf(NT, 128), P.buf("av%d" % i, dma=True)) for i in range(2)]
        NE = 4
        E = [(arena.bf(2, 512), P.buf("aE%d" % i)) for i in range(NE)]
        Es = [[(arena.bf(512), P.buf("aEs%d_%d" % (c, i))) for i in range(2)] for c in range(2)]
        rz = [(arena.f32(1024), P.buf("arz%d" % i)) for i in range(2)]
        oo = [(arena.f32(512), P.buf("aoo%d" % i)) for i in range(2)]
        ot = [(arena.f32(512), P.buf("aot%d" % i)) for i in range(2)]
        rs = [(arena.f32(512), P.buf("ars%d" % i)) for i in range(2)]
        sq = [(arena.bf(512), P.buf("asq%d" % i)) for i in range(2)]
        on = [(arena.bf(512), P.buf("aon%d" % i, dma=True)) for i in range(2)]
        SP_ = [(0, 1), (2, 3)]
        UB, ZB = [4, 5], [6, 7]

        def load(h):
            s = h % 2
            dma("sp", qh[s][0], q_d[h, :, 0:S], [B_qd[h]], [qh[s][1]], qh[s][1])
            dma("sp", kh[s][0], k_d[h, :, 0:S], [B_kd[h]], [kh[s][1]], kh[s][1])
            dma("sp", vh[s][0], v_d[0:S, h * 128:(h + 1) * 128].rearrange("(t p) e -> p t e", p=128), [B_vd], [vh[s][1]], vh[s][1])

        load(0)
        assert arena.off <= WOUT_OFF, arena.off
        load_F_weights(l)
        pend = []
        it = 0
        for h in range(4):
            if h + 1 < 4:
                load(h + 1)
            s = h % 2
            (q, Bq), (k, Bkk), (v, Bv) = qh[s], kh[s], vh[s]
            for qb in range(NB):
                p = it % 2
                it += 1

                def S_ops(kt, q=q, k=k, Bq=Bq, Bkk=Bkk, qb=qb):
                    b0, b1 = SP_[kt % 2]
                    Et, BE = E[kt % NE]
                    for c, bk in ((0, b0), (1, b1)):
                        P.op("pe", lambda e, c=c, bk=bk: e.matmul(banks[bk], lhsT=k[64 * c:64 * c + 64, kt * 128:(kt + 1) * 128],
                                                               rhs=q[64 * c:64 * c + 64, qb * 512:(qb + 1) * 512], start=True, stop=True),
                             reads=[Bq, Bkk], writes=[Bk[bk]])
                    src = bank2[b0 // 2].rearrange("p (c n) -> p c n", c=2)
                    P.op("act", lambda e: e.activation(out=Et, in_=src, func=AF.Exp, scale=0.125), reads=[Bk[b0], Bk[b1]], writes=[BE])

                def PV_ops(kt, v=v, Bv=Bv):
                    Et, BE = E[kt % NE]
                    for c in range(2):
                        P.op("pe", lambda e, c=c: e.matmul(banks[UB[c]], lhsT=v[:, kt, :], rhs=Et[:, c, :], start=(kt == 0), stop=(kt == NT - 1)),
                             reads=[Bv, BE], writes=[Bk[UB[c]]], acc=True)
                    if kt % 2 == 1:
                        Ep, BEp = E[(kt - 1) % NE]
                        for c, eng in ((0, "dve"), (1, "pool")):
                            es, Bes = Es[c][(kt // 2) % 2]
                            P.op(eng, lambda e, c=c, es=es: e.tensor_tensor(out=es, in0=Ep[:, c, :], in1=Et[:, c, :], op=ALU.add), reads=[BEp, BE], writes=[Bes])

                def Z_ops(j):
                    for c in range(2):
                        es, Bes = Es[c][j % 2]
                        P.op("pe", lambda e, c=c, es=es: e.matmul(banks[ZB[c]], lhsT=ones, rhs=es, start=(j == 0), stop=(j == NT // 2 - 1)),
                             reads=[B_ones, Bes], writes=[Bk[ZB[c]]], acc=True)

                S_ops(0)
                if NT > 1:
                    S_ops(1)
                for kt in range(NT):
                    if kt + 2 < NT:
                        S_ops(kt + 2)
                    PV_ops(kt)
                    if kt % 2 == 0 and kt >= 2:
                        Z_ops(kt // 2 - 1)
                    if kt == 3 and pend:
                        pend.pop(0)()
                Z_ops(NT // 2 - 1)
                (r01, Br01) = rz[p]
                (o0, Bo0), (o1, Bo1) = oo[0], oo[1]
                (o, Bo) = ot[p]
                (sqt, Bsq) = sq[p]
                zsrc = bank2[ZB[0] // 2]
                P.op("act", lambda e, r01=r01: e.activation(out=r01, in_=zsrc, func=AF.Ln), reads=[Bk[ZB[0]], Bk[ZB[1]]], writes=[Br01])
                P.op("act", lambda e, r01=r01: e.activation(out=r01, in_=r01, func=AF.Exp, scale=-1.0), reads=[Br01], writes=[Br01])
                r0, r1, Br0, Br1 = r01[:, 0:512], r01[:, 512:1024], Br01, Br01
                P.op("dve", lambda e, o0=o0, r0=r0: e.tensor_tensor(out=o0, in0=banks[UB[0]], in1=r0, op=ALU.mult), reads=[Bk[UB[0]], Br0], writes=[Bo0])
                P.op("dve", lambda e, o1=o1, r1=r1: e.tensor_tensor(out=o1, in0=banks[UB[1]], in1=r1, op=ALU.mult), reads=[Bk[UB[1]], Br1], writes=[Bo1])
                P.op("dve", lambda e, o=o, o0=o0, o1=o1: e.scalar_tensor_tensor(out=o, in0=o1, scalar=neglam[:, l:l + 1], in1=o0, op0=ALU.mult, op1=ALU.add),
                     reads=[Bo0, Bo1, B_lam], writes=[Bo])
                P.op("pool", lambda e, o=o, sqt=sqt: e.tensor_tensor(out=sqt, in0=o, in1=o, op=ALU.mult), reads=[Bo], writes=[Bsq])

                def part2(p=p, o=o, Bo=Bo, sqt=sqt, Bsq=Bsq, h=h, qb=qb):
                    nb_ = 0
                    (rst, Brs) = rs[p]
                    (ont, Bon) = on[p]
                    P.op("pe", lambda e: e.matmul(banks[nb_], lhsT=ones, rhs=sqt, start=True, stop=True), reads=[B_ones, Bsq], writes=[Bk[nb_]])
                    P.op("act", lambda e: e.activation(out=rst, in_=banks[nb_], func=AF.Ln, scale=1.0 / 128, bias=epsb[SUBLN_EPS]), reads=[Bk[nb_], B_eps], writes=[Brs])
                    P.op("act", lambda e: e.activation(out=rst, in_=rst, func=AF.Exp, scale=-0.5), reads=[Brs], writes=[Brs])
                    P.op("dve", lambda e: e.scalar_tensor_tensor(out=ont, in0=o, scalar=gs[:, l:l + 1], in1=rst, op0=ALU.mult, op1=ALU.mult),
                         reads=[Bo, Brs, B_lam], writes=[Bon])
                    dma("sp", mix_d[h, :, qb * 512:(qb + 1) * 512], ont, [Bon], [B_mix[qb]], Bon, acc=True)

                pend.append(part2)
        while pend:
            pend.pop(0)()
        P.barrier()

    def pass_F(job, l, xsrc, S, ydst, last):
        arena.reset()
        TB = min(1024, S)
        NTI = TB // 128
        NH = TB // 512
        NBLK = S // TB
        mixb = [(arena.bf(8, 256), P.buf("fmix%d" % i, dma=True)) for i in range(2)]
        xts = [(arena.f32(D_MODEL), P.buf("fxt%d" % i, dma=True)) for i in range(4)]
        x1s = [(arena.f32(D_MODEL), P.buf("fx1%d" % i, dma=True)) for i in range(2)]
        junk, Bj = arena.bf(D_MODEL), P.buf("fjunk")
        xns = [(arena.bf(D_MODEL), P.buf("fxn%d" % i)) for i in range(4)]
        xnT, BxnT = arena.bf(8, TB), P.buf("fxnT")
        hT, BhT = arena.bf(NJ, TB), P.buf("fhT")
        wgu = [(arena.bf(2, 8, 128), P.buf("fwgu%d" % i, dma=True)) for i in range(4)]
        sgt = [(arena.f32(512), P.buf("fsg%d" % i)) for i in range(2)]
        assert arena.off <= WOUT_OFF, arena.off
        dma("sp", g2b, n2g[l].partition_broadcast(128), [], [B_g2b], B_g2b)
        pairs = [(0, 1), (2, 3), (4, 5), (6, 7)]
        ctr = {"xt": 0, "x1": 0, "xn": 0, "w": 0, "p": 0, "t": 0, "s3": 0}

        def nxt(k, n):
            v = ctr[k] % n
            ctr[k] += 1
            return v

        def s1_front(blk, i):
            t = blk * NTI + i
            if i % 2 == 0:
                mb, Bmb = mixb[(t // 2) % 2]
                c0 = t * 128
                dma("sp", mb, mix_d[:, :, c0:c0 + 256].rearrange("c p n -> p c n"), [B_mix[t // 4]], [Bmb], Bmb)
            mb, Bmb = mixb[(t // 2) % 2]
            xt, Bxt = xts[nxt("xt", 4)]
            x1, Bx1 = x1s[nxt("x1", 2)]
            xn, Bxn = xns[nxt("xn", 4)]
            dma("sp", xt, xsrc[t * 128:(t + 1) * 128, :], [Bx_of(job, t)] if l > 0 else [], [Bxt], Bxt)
            for half, bk in ((0, 0), (1, 1)):
                for c in range(8):
                    P.op("pe", lambda e, bk=bk, c=c, half=half: e.matmul(banks[bk], lhsT=mb[:, c, (i % 2) * 128:(i % 2 + 1) * 128],
                                                                       rhs=Wout[:, c, half * 512:(half + 1) * 512], start=(c == 0), stop=(c == 7)),
                         reads=[Bmb, B_Wout], writes=[Bk[bk]], acc=True)
                P.op("dve", lambda e, bk=bk, half=half: e.tensor_tensor(out=x1[:, half * 512:(half + 1) * 512], in0=banks[bk],
                                                                      in1=xt[:, half * 512:(half + 1) * 512], op=ALU.add),
                     reads=[Bk[bk], Bxt], writes=[Bx1], acc=(half == 1))
            dma("sp", xres[jrow[job] + t * 128: jrow[job] + (t + 1) * 128, :], x1, [Bx1], [Bx_of(job, t)], Bx1)
            norm_tile(x1, Bx1, g2b, B_g2b, xn, Bxn, junk, Bj)
            return (xn, Bxn, i)

        def s1_back(item):
            xn, Bxn, i = item
            transpose_tile(xn, Bxn, 2 + nxt("t", 2), xnT[:, :, i * 128:(i + 1) * 128], BxnT, "act" if i % 2 == 0 else "dve")

        def stage2(blk):
            for j in range(NJ):
                wt, Bwt = wgu[nxt("w", 4)]
                dma("sp", wt, wgu_b[l, j], [B_wgu_d], [Bwt], Bwt)
                for half in range(NH):
                    bg, bu = pairs[nxt("p", 4)]
                    for (bk, gi) in ((bg, 0), (bu, 1)):
                        for kc in range(8):
                            P.op("pe", lambda e, bk=bk, gi=gi, kc=kc, wt=wt, half=half: e.matmul(banks[bk], lhsT=wt[:, gi, kc, :], rhs=xnT[:, kc, half * 512:(half + 1) * 512],
                                                                             start=(kc == 0), stop=(kc == 7)),
                                 reads=[Bwt, BxnT], writes=[Bk[bk]], acc=True)
                    sg, Bsg = sgt[ctr["p"] % 2]
                    P.op("act", lambda e, bg=bg, sg=sg: e.activation(out=sg, in_=banks[bg], func=AF.Silu), reads=[Bk[bg]], writes=[Bsg])
                    P.op("dve", lambda e, bu=bu, sg=sg, j=j, half=half: e.tensor_tensor(out=hT[:, j, half * 512:(half + 1) * 512], in0=banks[bu], in1=sg, op=ALU.mult),
                         reads=[Bk[bu], Bsg], writes=[BhT], acc=True)

        def s3_tile(blk, i):
            t = blk * NTI + i
            xt, Bxt = xts[nxt("xt", 4)]
            x2, Bx2 = x1s[nxt("x1", 2)]
            dma("sp", xt, xres[jrow[job] + t * 128: jrow[job] + (t + 1) * 128, :], [Bx_of(job, t)], [Bxt], Bxt)
            ba, bb = pairs[2 + nxt("s3", 2)]
            for half, bk in ((0, ba), (1, bb)):
                for j in range(NJ):
                    P.op("pe", lambda e, bk=bk, j=j, half=half: e.matmul(banks[bk], lhsT=hT[:, j, i * 128:(i + 1) * 128], rhs=Wd[:, j, half * 512:(half + 1) * 512],
                                                                       start=(j == 0), stop=(j == NJ - 1)),
                         reads=[BhT, B_Wd], writes=[Bk[bk]], acc=True)
                P.op("dve", lambda e, bk=bk, half=half: e.tensor_tensor(out=x2[:, half * 512:(half + 1) * 512], in0=banks[bk],
                                                                      in1=xt[:, half * 512:(half + 1) * 512], op=ALU.add),
                     reads=[Bk[bk], Bxt], writes=[Bx2], acc=(half == 1))
            if not last:
                dma("sp", xres[jrow[job] + t * 128: jrow[job] + (t + 1) * 128, :], x2, [Bx2], [Bx_of(job, t)], Bx2)
            else:
                col, Bc = smallcol()
                P.op("act", lambda e: e.activation(out=junk, in_=x2, func=AF.Square, accum_out=col[:, 0:1]), reads=[Bx2], writes=[Bj, Bc])
                P.op("act", lambda e: e.activation(out=col[:, 1:2], in_=col[:, 0:1], func=AF.Ln, scale=1.0 / D_MODEL, bias=epsb[NORM_EPS]),
                     reads=[Bc, B_eps], writes=[Bc])
                P.op("act", lambda e: e.activation(out=col[:, 2:3], in_=col[:, 1:2], func=AF.Exp, scale=-0.5), reads=[Bc], writes=[Bc])
                P.op("dve", lambda e: e.scalar_tensor_tensor(out=xt, in0=x2, scalar=col[:, 2:3], in1=gfb, op0=ALU.mult, op1=ALU.mult),
                     reads=[Bx2, Bc, B_gfb], writes=[Bxt])
                dma("sp", ydst[t * 128:(t + 1) * 128, :], xt, [Bxt], [], Bxt)

        LAG = 2
        backs = []
        for i in range(NTI):
            backs.append(s1_front(0, i))
            if len(backs) > LAG:
                s1_back(backs.pop(0))
        while backs:
            s1_back(backs.pop(0))
        if job == 0 and l + 1 < depth:
            prep_layer(l + 1)
        for blk in range(NBLK):
            stage2(blk)
            for i in range(NTI):
                s3_tile(blk, i)
                if blk + 1 < NBLK:
                    backs.append(s1_front(blk + 1, i))
                    if len(backs) > LAG:
                        s1_back(backs.pop(0))
            while backs:
                s1_back(backs.pop(0))
        P.barrier()

    jrow = []
    r = 0
    for (_, _, S) in jobspec:
        jrow.append(r)
        r += S
    for job, (nm, r0, S) in enumerate(jobspec):
        for l in range(depth):
            xsrc = xin[nm][r0:r0 + S, :] if l == 0 else xres[jrow[job]:jrow[job] + S, :]
            pass_P(job, l, xsrc, S)
            pass_D(job, l, S)
            pass_A(job, l, S)
            pass_F(job, l, xsrc, S, yout[nm][r0:r0 + S, :], l == depth - 1)
    P.emit()
    return nc


def _const_tables(smax, svals):
    bf = ml_dtypes.bfloat16
    inv = (1.0 / (10000.0 ** (np.arange(0, 64, 2, dtype=np.float32) / np.float32(64)))).astype(np.float32)
    pos = np.arange(smax, dtype=np.float32)
    p = np.arange(128)
    ang = (pos[None, :] * inv[p % 32][:, None]).astype(np.float32)
    rcos = np.cos(ang).astype(np.float32)
    sign = np.where((p % 64) < 32, -1.0, 1.0).astype(np.float32)
    rsin = (np.sin(ang) * sign[:, None]).astype(np.float32)
    c = np.arange(128)
    a = 2.0 * np.pi * ((c[:, None] * c[None, :]) % 128) / 128.0
    chc = (np.cos(a) / np.sqrt(128.0)).astype(bf)
    chs = (-np.sin(a) / np.sqrt(128.0)).astype(bf)
    out = {"rcos": rcos, "rsin": rsin, "chc": chc, "chs": chs}
    for S in svals:
        H = S // 2
        t = np.arange(H, dtype=np.int64)
        tabs = []
        for par in range(2):
            m = (t[:, None] * (2 * t[None, :] + par)) % S
            a = (2.0 * np.pi / S) * m.astype(np.float64)
            tabs.append((np.cos(a) / np.sqrt(S)).astype(np.float32).astype(bf))
            tabs.append((np.sin(a) / np.sqrt(S)).astype(np.float32).astype(bf))
        out["dft%d" % S] = np.stack(tabs, axis=0)
    return out


def _swap_cols():
    idx = []
    for n in range(8):
        idx += list(range(n * 64 + 32, n * 64 + 64)) + list(range(n * 64, n * 64 + 32))
    return np.array(idx)


def _shared_inputs(depth, norm1_g, w_in, lambda_q1, lambda_k1, lambda_q2, lambda_k2, subln_g, w_out, norm2_g,
                   w_gate, w_up, w_down, final_g, smax, svals):
    sw = _swap_cols()
    w_in = np.asarray(w_in, np.float32)
    w_inx = w_in
    lamv = np.stack([lambda_q1, lambda_k1, lambda_q2, lambda_k2], axis=1).astype(np.float32).reshape(-1)
    d = {
        "w_inx": np.ascontiguousarray(w_inx), "w_out": np.asarray(w_out, np.float32), "w_gate": np.asarray(w_gate, np.float32),
        "w_up": np.asarray(w_up, np.float32), "w_down": np.asarray(w_down, np.float32),
        "n1g": np.asarray(norm1_g, np.float32), "n2g": np.asarray(norm2_g, np.float32), "fg": np.asarray(final_g, np.float32),
        "lamv": np.ascontiguousarray(lamv), "sg": np.ascontiguousarray(np.asarray(subln_g, np.float32).T),
    }
    d.update(_const_tables(smax, svals))
    return d


_NC_CACHE = {}


def kernel(x_prompt, x_sample, norm1_g, w_in, lambda_q1, lambda_k1, lambda_q2, lambda_k2,
           subln_g, w_out, norm2_g, w_gate, w_up, w_down, final_g):
    x_prompt = np.asarray(x_prompt, np.float32)
    x_sample = np.asarray(x_sample, np.float32)
    depth = int(np.asarray(w_in).shape[0])
    n = 8
    Bp, Sp, _ = x_prompt.shape
    Bs, Ss, _ = x_sample.shape
    ppc, spc = Bp // n, Bs // n
    jobspec = [("xp", i * Sp, Sp) for i in range(ppc)] + [("xs", i * Ss, Ss) for i in range(spc)]
    smax = max(Sp, Ss)
    svals = sorted({Sp, Ss})
    key = (depth, tuple(jobspec))
    if key not in _NC_CACHE:
        _NC_CACHE[key] = build_program(depth, jobspec, smax)
    nc = _NC_CACHE[key]
    shared = _shared_inputs(depth, norm1_g, w_in, lambda_q1, lambda_k1, lambda_q2, lambda_k2, subln_g, w_out, norm2_g,
                            w_gate, w_up, w_down, final_g, smax, svals)
    in_maps = []
    for c in range(n):
        m = dict(shared)
        m["xp"] = np.ascontiguousarray(x_prompt[c * ppc:(c + 1) * ppc].reshape(ppc * Sp, D_MODEL))
        m["xs"] = np.ascontiguousarray(x_sample[c * spc:(c + 1) * spc].reshape(spc * Ss, D_MODEL))
        in_maps.append(m)
    res = run_bass_kernel_spmd(nc, in_maps, core_ids=list(range(n)))
    yp = np.concatenate([np.asarray(r["y_xp"]).reshape(ppc, Sp, D_MODEL) for r in res.results], axis=0)
    ys = np.concatenate([np.asarray(r["y_xs"]).reshape(spc, Ss, D_MODEL) for r in res.results], axis=0)
    return (yp.astype(np.float32), ys.astype(np.float32))
```
